# Optimizing a Trainium2 kernel written in Bass

```python
import jax, jax.numpy as jnp
from jax import lax
import numpy as np

D_MODEL = 2048
BATCH = 2
SEQ = 16384
DEPTH = 1

HEAD_DIM = 128
N_HEADS_FOX = 8
N_HEADS_SB = 8
WIDTH_FOX = N_HEADS_FOX * HEAD_DIM
WIDTH_SB = N_HEADS_SB * HEAD_DIM
D_FF = 4 * D_MODEL
Q_BLOCK = 128
RMS_EPS = 1e-6
NEG_INF = -1e30
IN_COLS = 3 * WIDTH_FOX + N_HEADS_FOX + 3 * WIDTH_SB + 2 * D_MODEL

kernel_name = "hybrid_fox_stickbreaking_gated_block"


def rmsnorm(x, g):
    xf = x.astype(jnp.float32)
    var = jnp.mean(xf * xf, axis=-1, keepdims=True)
    return (xf * lax.rsqrt(var + RMS_EPS) * g.astype(jnp.float32)).astype(x.dtype)


def split_heads(t, n_heads):
    b, s, _ = t.shape
    return t.reshape(b, s, n_heads, HEAD_DIM).transpose(0, 2, 1, 3)


def merge_heads(t):
    b, n, s, d = t.shape
    return t.transpose(0, 2, 1, 3).reshape(b, s, n * d)


def to_query_blocks(t):
    b, h, s = t.shape[:3]
    rest = t.shape[3:]
    t = t.reshape((b, h, s // Q_BLOCK, Q_BLOCK) + rest)
    return jnp.moveaxis(t, 2, 0)


def from_query_blocks(t):
    nb, b, h, q, d = t.shape
    return jnp.moveaxis(t, 0, 2).reshape(b, h, nb * q, d)


def forgetting_attention(q, k, v, log_f):
    seq = q.shape[2]
    scale = HEAD_DIM ** -0.5
    c = lax.cumsum(log_f, axis=2)
    key_pos = jnp.arange(seq)
    starts = jnp.arange(seq // Q_BLOCK) * Q_BLOCK

    def one_block(args):
        q_blk, c_blk, start = args
        q_pos = start + jnp.arange(Q_BLOCK)
        logits = jnp.einsum("bhqd,bhkd->bhqk", q_blk, k) * scale
        logits = logits + c_blk[..., None] - c[:, :, None, :]
        causal = key_pos[None, :] <= q_pos[:, None]
        logits = jnp.where(causal, logits, NEG_INF)
        p = jax.nn.softmax(logits, axis=-1)
        return jnp.einsum("bhqk,bhkd->bhqd", p, v)

    out = lax.map(one_block, (to_query_blocks(q), to_query_blocks(c), starts))
    return from_query_blocks(out)


def stick_breaking_attention(q, k, v):
    seq = q.shape[2]
    scale = HEAD_DIM ** -0.5
    key_pos = jnp.arange(seq)
    starts = jnp.arange(seq // Q_BLOCK) * Q_BLOCK

    def one_block(args):
        q_blk, start = args
        q_pos = start + jnp.arange(Q_BLOCK)
        z = jnp.einsum("bhqd,bhkd->bhqk", q_blk, k) * scale
        strict = key_pos[None, :] < q_pos[:, None]
        log_not_beta = jnp.where(strict, jax.nn.log_sigmoid(-z), 0.0)
        after = lax.cumsum(log_not_beta, axis=3, reverse=True) - log_not_beta
        log_a = jax.nn.log_sigmoid(z) + after
        a = jnp.where(strict, jnp.exp(log_a), 0.0)
        return jnp.einsum("bhqk,bhkd->bhqd", a, v)

    out = lax.map(one_block, (to_query_blocks(q), starts))
    return from_query_blocks(out)


def setup_inputs(seed: int = 0) -> dict:
    key = jax.random.key(seed)
    ks = jax.random.split(key, 12)

    def dense(k, fan_in, fan_out):
        return jax.random.normal(k, (DEPTH, fan_in, fan_out), jnp.float32) * fan_in ** -0.5

    def gain(k, n):
        return 1.0 + 0.02 * jax.random.normal(k, (DEPTH, n), jnp.float32)

    x = jax.random.normal(ks[0], (BATCH, SEQ, D_MODEL), jnp.float32)
    norm_mix_g = gain(ks[1], D_MODEL)
    w_in = dense(ks[2], D_MODEL, IN_COLS)
    b_forget = 2.0 + 0.5 * jax.random.normal(ks[3], (DEPTH, N_HEADS_FOX), jnp.float32)
    w_out_fox = dense(ks[4], WIDTH_FOX, D_MODEL)
    w_out_sb = dense(ks[5], WIDTH_SB, D_MODEL)
    w_out = dense(ks[6], D_MODEL, D_MODEL)
    norm_mlp_g = gain(ks[7], D_MODEL)
    w_mlp_up = dense(ks[8], D_MODEL, D_FF)
    w_mlp_down = dense(ks[9], D_FF, D_MODEL)
    norm_final_g = 1.0 + 0.02 * jax.random.normal(ks[10], (D_MODEL,), jnp.float32)
    return {"x": x, "norm_mix_g": norm_mix_g, "w_in": w_in, "b_forget": b_forget,
            "w_out_fox": w_out_fox, "w_out_sb": w_out_sb, "w_out": w_out,
            "norm_mlp_g": norm_mlp_g, "w_mlp_up": w_mlp_up, "w_mlp_down": w_mlp_down,
            "norm_final_g": norm_final_g}


def reference(x, norm_mix_g, w_in, b_forget, w_out_fox, w_out_sb, w_out,
              norm_mlp_g, w_mlp_up, w_mlp_down, norm_final_g):
    dt = x.dtype
    split_points = np.cumsum([WIDTH_FOX, WIDTH_FOX, WIDTH_FOX, N_HEADS_FOX,
                              WIDTH_SB, WIDTH_SB, WIDTH_SB, D_MODEL])
    for l in range(DEPTH):
        xn = rmsnorm(x, norm_mix_g[l])
        proj = xn @ w_in[l]
        q_a, k_a, v_a, f_a, q_b, k_b, v_b, g_a, g_b = jnp.split(proj, split_points, axis=-1)

        log_f = jax.nn.log_sigmoid((f_a + b_forget[l]).astype(jnp.float32))
        log_f = log_f.transpose(0, 2, 1)
        y_a = forgetting_attention(
            split_heads(q_a, N_HEADS_FOX).astype(jnp.float32),
            split_heads(k_a, N_HEADS_FOX).astype(jnp.float32),
            split_heads(v_a, N_HEADS_FOX).astype(jnp.float32),
            log_f)
        y_a = merge_heads(y_a).astype(dt) @ w_out_fox[l]

        y_b = stick_breaking_attention(
            split_heads(q_b, N_HEADS_SB).astype(jnp.float32),
            split_heads(k_b, N_HEADS_SB).astype(jnp.float32),
            split_heads(v_b, N_HEADS_SB).astype(jnp.float32))
        y_b = merge_heads(y_b).astype(dt) @ w_out_sb[l]

        merged = jax.nn.sigmoid(g_a) * y_a + jax.nn.sigmoid(g_b) * y_b
        x = x + merged @ w_out[l]

        h = rmsnorm(x, norm_mlp_g[l])
        u = jax.nn.relu(h @ w_mlp_up[l])
        x = x + (u * u) @ w_mlp_down[l]

    return rmsnorm(x, norm_final_g)
```

```python
import contextlib
import numpy as np
import ml_dtypes
import concourse.bass as bass
import concourse.mybir as mybir
from concourse.bass_utils import run_bass_kernel_spmd

F32 = mybir.dt.float32
BF16 = mybir.dt.bfloat16
AF = mybir.ActivationFunctionType
ALU = mybir.AluOpType

ENGS = ("pe", "act", "dve", "pool", "sp")
EPOCH = 20000
DMA_EPOCH = 1500
NEG = -30000.0
RMS_EPS = 1e-6


class Buf:
    __slots__ = ("name", "writers", "readers")

    def __init__(self, name):
        self.name = name
        self.writers = []
        self.readers = []


class Op:
    __slots__ = ("eng", "fn", "is_dma", "key", "deps", "flag", "cnt", "sem", "barrier")

    def __init__(self, eng, fn, is_dma=False, key=None):
        self.eng = eng
        self.fn = fn
        self.is_dma = is_dma
        self.key = key
        self.deps = []
        self.flag = False
        self.cnt = None
        self.sem = None
        self.barrier = None


class Prog:
    def __init__(self, nc):
        self.nc = nc
        self.ops = {e: [] for e in ENGS}
        self.dma_keys = {}
        self.nbuf = 0
        self.n_ops = 0

    def buf(self, name=None):
        self.nbuf += 1
        return Buf(name or f"b{self.nbuf}")

    def bufs(self, n, name="b"):
        return [self.buf(f"{name}{i}") for i in range(n)]

    def _track(self, op, reads, writes):
        deps = []
        for b in reads:
            deps.extend(b.writers)
            b.readers.append(op)
        for b in writes:
            if b.readers:
                deps.extend(b.readers)
                deps.extend(b.writers)
                b.writers = [op]
                b.readers = []
            elif b.writers and not all(w.eng == op.eng for w in b.writers):
                deps.extend(b.writers)
                b.writers = [op]
            elif op.is_dma or any(w.is_dma for w in b.writers):
                b.writers.append(op)
            else:
                b.writers = [op]
        out = []
        seen = set()
        for d in deps:
            if d is op or id(d) in seen:
                continue
            seen.add(id(d))
            out.append(d)
        return out

    def op(self, eng, fn, reads=(), writes=()):
        o = Op(eng, fn)
        raw = set()
        for b in reads:
            for w in b.writers:
                raw.add(id(w))
        deps = self._track(o, reads, writes)
        keep = []
        for d in deps:
            if (not d.is_dma) and d.eng == eng:
                if eng == "pe" or id(d) not in raw:
                    continue
            keep.append(d)
            d.flag = True
        o.deps = keep
        self.ops[eng].append(o)
        self.n_ops += 1
        return o

    def dma(self, eng, out, in_, key, reads=(), writes=(), **kw):
        def fn(e):
            return e.dma_start(out=out, in_=in_, **kw)
        o = Op(eng, fn, is_dma=True, key=key)
        o.deps = self._track(o, reads, writes)
        for d in o.deps:
            d.flag = True
        self.ops[eng].append(o)
        self.dma_keys.setdefault(key, []).append(o)
        self.n_ops += 1
        return o

    def barrier(self):
        targets = []
        for e in ENGS:
            for o in reversed(self.ops[e]):
                if (not o.is_dma) and o.barrier is None:
                    o.flag = True
                    targets.append(o)
                    break
        for lst in self.dma_keys.values():
            if lst:
                targets.append(lst[-1])
        for e in ENGS:
            o = Op(e, None)
            o.barrier = list(targets)
            self.ops[e].append(o)

    def emit(self):
        nc = self.nc
        sems = {}

        def get_sem(name):
            if name not in sems:
                sems[name] = nc.alloc_semaphore(name=name)
            return sems[name]

        for e in ENGS:
            c = 0
            ep = 0
            for o in self.ops[e]:
                if o.is_dma or o.barrier is not None:
                    continue
                if o.flag:
                    c += 1
                    if c > EPOCH:
                        ep += 1
                        c = 1
                    o.sem = f"c_{e}_{ep}"
                    o.cnt = c
        for key, lst in self.dma_keys.items():
            c = 0
            ep = 0
            for o in lst:
                c += 1
                if c > DMA_EPOCH:
                    ep += 1
                    c = 1
                o.sem = f"d_{key}_{ep}"
                o.cnt = 16 * c

        def emit_engine(ename, e):
            waited = {}
            for o in self.ops[ename]:
                deps = o.barrier if o.barrier is not None else o.deps
                need = {}
                for d in deps:
                    if d.sem is None:
                        continue
                    if need.get(d.sem, 0) < d.cnt:
                        need[d.sem] = d.cnt
                for s, v in need.items():
                    if waited.get(s, 0) >= v:
                        continue
                    e.wait_ge(get_sem(s), v)
                    waited[s] = v
                if o.barrier is not None:
                    continue
                inst = o.fn(e)
                if o.is_dma:
                    inst.then_inc(get_sem(o.sem), 16)
                elif o.flag:
                    inst.then_inc(get_sem(o.sem), 1)

        with nc.Block() as block:
            @block.tensor
            def _(e):
                emit_engine("pe", e)

            @block.scalar
            def _(e):
                emit_engine("act", e)

            @block.vector
            def _(e):
                emit_engine("dve", e)

            @block.gpsimd
            def _(e):
                emit_engine("pool", e)

            @block.sync
            def _(e):
                emit_engine("sp", e)
        self.n_sems = len(sems)


FULL = dict(D=2048, H=8, DFF=8192, NSLOT=8)


def build(cfg):
    D = cfg["D"]; H = cfg["H"]; DFF = cfg["DFF"]; NSLOT = cfg["NSLOT"]
    KC = D // 128; W = H * 128; JC = DFF // 128
    S = 2048 * NSLOT; NB = S // 512; NT = S // 128; NOWN = 512 * NSLOT
    GW = 256
    HJ = JC // 2
    assert HJ <= 32 and HJ % KC == 0 or HJ <= KC
    IN_COLS = 6 * W + H + 2 * D
    oqf = 0; okf = W; ovf = 2 * W; off_ = 3 * W
    oqs = 3 * W + H; oks = oqs + W; ovs = oks + W; oga = ovs + W; ogb = oga + D
    SCALE = 128 ** -0.5

    nc = bass.Bass("TRN2", target_bir_lowering=False)
    P = Prog(nc)

    def din(name, shape, dt=F32):
        return nc.dram_tensor(name, list(shape), dt, kind="ExternalInput").ap()

    def dscr(name, shape, dt=BF16):
        return nc.dram_tensor(name, list(shape), dt, kind="Internal").ap()

    x_seq = din("x_seq", [S, D]); x_own = din("x_own", [NOWN, D])
    w_in = din("w_in", [D, IN_COLS]); w_of = din("w_of", [W, D]); w_os = din("w_os", [W, D])
    w_o = din("w_o", [D, D]); w_up = din("w_up", [D, DFF]); w_dn = din("w_dn", [DFF, D])
    gmix_d = din("gmix", [128, KC]); gmlp_d = din("gmlp", [128, KC]); gfin_d = din("gfin", [128, KC])
    bfg_d = din("bfg", [128, H])
    masks_d = din("masks", [128, 16, 513], BF16)
    sel_d = din("sel", [16, 4], F32)
    cbf_d = din("cbf", [128, 4, 128], BF16)
    cf32_d = din("cf32", [128, 3, 128], F32)
    out_d = nc.dram_tensor("out", [NOWN, D], F32, kind="ExternalOutput").ap()

    NGQ = 2 * W // GW; NGD = D // GW; NGU = DFF // GW
    wq_s = dscr("wq_s", [NGQ, 128, KC, GW]); wga_s = dscr("wga_s", [NGD, 128, KC, GW])
    wgb_s = dscr("wgb_s", [NGD, 128, KC, GW]); wof_s = dscr("wof_s", [NGD, 128, H, GW])
    wos_s = dscr("wos_s", [NGD, 128, H, GW]); wo_s = dscr("wo_s", [NGD, 128, KC, GW])
    wup_s = dscr("wup_s", [NGU, 128, KC, GW]); wdn_s = dscr("wdn_s", [NGD, 128, JC, GW])
    KT_s = dscr("KT_s", [2, H, 128, S]); V_s = dscr("V_s", [2, S, W])

    es = contextlib.ExitStack()

    def sb(name, shape, dt, stack=None):
        return (stack or es).enter_context(nc.sbuf_tensor("s_" + name, list(shape), dt))

    with es:
        PSZ = [es.enter_context(nc.psum_tensor(f"psz{i}", [128, 1024], F32)) for i in range(2)]
        PSa = [es.enter_context(nc.psum_tensor(f"ps{i}", [128, 512], F32)) for i in range(4)]
        PS = [p[:, :] for p in PSa] + [PSZ[0][:, 0:512], PSZ[0][:, 512:1024], PSZ[1][:, 0:512], PSZ[1][:, 512:1024]]
        PB = P.bufs(8, "psb")
        rr = [0]

        def nbank():
            i = rr[0] % 8
            rr[0] += 1
            return i

        cbf = sb("cbf", [128, 4, 128], BF16); cf32 = sb("cf32", [128, 3, 128], F32)
        negselc = sb("negselc", [16, 4], F32)
        gmix = sb("gmix", [128, KC], F32); gmlp = sb("gmlp", [128, KC], F32); gfin = sb("gfin", [128, KC], F32)
        bfg = sb("bfg", [128, H], F32)
        NCt = sb("NCt", [128, NT, H], F32); offs = sb("offs", [128, NT, H], F32)
        fl = sb("fl", [128, NT, H], F32)
        b_const = P.buf("const"); b_NC = P.buf("NC"); b_offs = P.buf("offs"); b_fl = P.buf("fl")
        ident_bf = cbf[:, 0, :]; negtri = cbf[:, 1, :]; negones = cbf[:, 2, :]; ones_bf = cbf[:, 3, :]
        ident_f = cf32[:, 0, :]; tri_f = cf32[:, 1, :]; ones_f = cf32[:, 2, :]

        for dst, src in ((cbf, cbf_d), (cf32, cf32_d), (negselc, sel_d), (gmix, gmix_d), (gmlp, gmlp_d),
                         (gfin, gfin_d), (bfg, bfg_d)):
            P.dma("sp", dst[:], src, key="const", writes=[b_const])

        def mm(out, lhsT, rhs, start, stop, reads, writes, **kw):
            P.op("pe", lambda e: e.matmul(out, lhsT, rhs, start=start, stop=stop, **kw), reads, writes)

        def tr(out, in_, ident, reads, writes):
            P.op("pe", lambda e: e.transpose(out, in_, ident), reads, writes)

        def act(out, in_, func, reads, writes, **kw):
            P.op("act", lambda e: e.activation(out, in_, func, **kw), reads, writes)

        def ve(eng, name, reads, writes, *a, **kw):
            P.op(eng, lambda e: getattr(e, name)(*a, **kw), reads, writes)

        b_wscr = P.buf("wscr")

        def cast_groups(w_ap, r0, nrows, c0, ncols, dst, gbase):
            kcx = nrows // 128
            step = 16 if kcx > 16 else kcx
            for g in range(ncols // GW):
                for k0 in range(0, kcx, step):
                    src = w_ap[r0 + k0 * 128: r0 + (k0 + step) * 128, c0 + g * GW: c0 + (g + 1) * GW]
                    src = src.rearrange("(kc f) c -> f kc c", f=128)
                    P.dma("pool", dst[gbase + g, :, k0:k0 + step, :], src, key="cast", writes=[b_wscr])

        with contextlib.ExitStack() as s1:
            wkv = sb("wkv", [128, KC, 2 * W + 8], BF16, s1)
            xst = [sb(f"xst{i}", [128, D], F32, s1) for i in range(4)]
            xs = [sb(f"xs{i}", [128, D], BF16, s1) for i in range(4)]
            junk = sb("junk", [128, D], BF16, s1)
            ssq = sb("ssq", [128, 8], F32, s1); rstd = sb("rstd", [128, 8], F32, s1)
            xnT = [sb(f"xnT{i}", [128, KC, 512], BF16, s1) for i in range(2)]
            kst = [sb(f"kst{i}", [128, H, 512], BF16, s1) for i in range(2)]
            vst = [sb(f"vst{i}", [128, 4, W], BF16, s1) for i in range(2)]
            b_wkv = P.buf("wkv"); b_xst = P.bufs(4, "xst"); b_xs = P.bufs(4, "xs"); b_junk = P.buf("junk")
            b_ssq = P.bufs(8, "ssq"); b_rstd = P.bufs(8, "rstd"); b_xnT = P.bufs(2, "xnT")
            b_kst = P.bufs(2, "kst"); b_vst = P.bufs(2, "vst"); b_KV = P.buf("KVs")

            cnt = [0]

            def load_wkv(mx):
                ok_, ov_ = (okf, ovf) if mx == 0 else (oks, ovs)
                for (c0, dcol, n) in ((ok_, 0, W), (ov_, W, W)) + (((off_, 2 * W, H),) if mx == 0 else ()):
                    for k0 in range(0, KC, 4):
                        k1 = min(KC, k0 + 4)
                        src = w_in[k0 * 128:k1 * 128, c0:c0 + n].rearrange("(kc f) c -> f kc c", f=128)
                        P.dma("pool", wkv[:, k0:k1, dcol:dcol + n], src, key="wkv", writes=[b_wkv])

            def pre(gb):
                blk = gb % NB
                for sub in range(4):
                    i = cnt[0] % 4; j = cnt[0] % 8; cnt[0] += 1
                    r0 = blk * 512 + sub * 128
                    P.dma("sp", xst[i][:], x_seq[r0:r0 + 128, :], key=f"xst{i}", writes=[b_xst[i]])
                    act(junk[:], xst[i][:], AF.Square, [b_xst[i]], [b_junk, b_ssq[j]],
                        accum_out=ssq[:, j:j + 1])
                    ve("dve", "tensor_scalar", [b_ssq[j]], [b_rstd[j]], rstd[:, j:j + 1], ssq[:, j:j + 1],
                       1.0 / D, RMS_EPS, ALU.mult, ALU.add)
                    act(rstd[:, j:j + 1], rstd[:, j:j + 1], AF.Sqrt, [b_rstd[j]], [b_rstd[j]])
                    ve("dve", "reciprocal", [b_rstd[j]], [b_rstd[j]], rstd[:, j:j + 1], rstd[:, j:j + 1])
                    ve("dve", "tensor_scalar", [b_xst[i], b_rstd[j]], [b_xs[sub]], xs[sub][:], xst[i][:],
                       rstd[:, j:j + 1], None, ALU.mult)

            def trans(gb):
                xn = xnT[gb % 2]; bxn = b_xnT[gb % 2]
                for kc in range(KC):
                    bk = nbank()
                    for sub in range(4):
                        mm(PS[bk][:, sub * 128:(sub + 1) * 128], xs[sub][:, kc * 128:(kc + 1) * 128],
                           ident_bf, True, True, [b_xs[sub], b_const], [PB[bk]], skip_group_check=True)
                    act(xn[:, kc, :], PS[bk][:, 0:512], AF.Copy, [PB[bk], b_const], [bxn],
                        scale=gmix[:, kc:kc + 1])

            def kproj(gb):
                mx = gb // NB; blk = gb % NB
                xn = xnT[gb % 2]; bxn = b_xnT[gb % 2]
                ks = kst[gb % 2]; bks = b_kst[gb % 2]
                for h in range(H):
                    bk = nbank()
                    for kc in range(KC):
                        mm(PS[bk][:, :], wkv[:, kc, h * 128:(h + 1) * 128], xn[:, kc, :], kc == 0, kc == KC - 1,
                           [b_wkv, bxn], [PB[bk]])
                    ve("dve", "tensor_copy", [PB[bk]], [bks], ks[:, h, :], PS[bk][:, :])
                P.dma("pool", KT_s[mx, :, :, blk * 512:(blk + 1) * 512].rearrange("h d t -> d h t"), ks[:],
                      key=f"kst{gb % 2}", reads=[bks], writes=[b_KV])

            def vproj(gb):
                mx = gb // NB; blk = gb % NB
                xn = xnT[gb % 2]; bxn = b_xnT[gb % 2]
                vs = vst[gb % 2]; bvs = b_vst[gb % 2]
                for sub in range(4):
                    for cg in range(W // 512):
                        bk = nbank()
                        for kc in range(KC):
                            mm(PS[bk][:, :], xn[:, kc, sub * 128:(sub + 1) * 128],
                               wkv[:, kc, W + cg * 512: W + (cg + 1) * 512], kc == 0, kc == KC - 1,
                               [b_wkv, bxn], [PB[bk]])
                        if (sub + cg) % 2 == 0:
                            act(vs[:, sub, cg * 512:(cg + 1) * 512], PS[bk][:, :], AF.Copy, [PB[bk]], [bvs])
                        else:
                            ve("dve", "tensor_copy", [PB[bk]], [bvs], vs[:, sub, cg * 512:(cg + 1) * 512],
                               PS[bk][:, :])
                P.dma("pool", V_s[mx, blk * 512:(blk + 1) * 512, :].rearrange("(s p) c -> p s c", p=128),
                      vs[:], key=f"vst{gb % 2}", reads=[bvs], writes=[b_KV])
                if mx == 0:
                    for sub in range(4):
                        bk = nbank()
                        for kc in range(KC):
                            mm(PS[bk][:, 0:H], xn[:, kc, sub * 128:(sub + 1) * 128],
                               wkv[:, kc, 2 * W:2 * W + H], kc == 0, kc == KC - 1, [b_wkv, bxn], [PB[bk]])
                        ve("dve", "tensor_tensor", [PB[bk], b_const], [b_fl], fl[:, blk * 4 + sub, :],
                           PS[bk][:, 0:H], bfg[:, :], ALU.add)

            load_wkv(0)
            pre(0)
            for i in range(2):
                cast_groups(w_in, 0, D, (oqf, oqs)[i], W, wq_s, i * (W // GW))
            cast_groups(w_in, 0, D, oga, D, wga_s, 0)
            cast_groups(w_in, 0, D, ogb, D, wgb_s, 0)
            cast_groups(w_of, 0, W, 0, D, wof_s, 0)
            cast_groups(w_os, 0, W, 0, D, wos_s, 0)
            cast_groups(w_o, 0, D, 0, D, wo_s, 0)
            cast_groups(w_up, 0, D, 0, DFF, wup_s, 0)
            cast_groups(w_dn, 0, DFF, 0, D, wdn_s, 0)
            trans(0)
            for gb in range(2 * NB):
                if gb == NB:
                    load_wkv(1)
                if gb + 1 < 2 * NB:
                    pre(gb + 1)
                kproj(gb)
                if gb + 1 < 2 * NB:
                    trans(gb + 1)
                vproj(gb)
        P.barrier()

        NF = NT * H
        with contextlib.ExitStack() as s2:
            e1 = sb("e1", [128, NF], F32, s2); l1 = sb("l1", [128, NF], F32, s2)
            cw = sb("cw", [128, NT, H], F32, s2)
            sa = sb("sa", [128, NT, H], F32, s2); sbb = sb("sbb", [128, NT, H], F32, s2)
            b_e1 = P.buf(); b_l1 = P.buf(); b_cw = P.buf(); b_sa = P.buf(); b_sb = P.buf()
            flf = fl[:].rearrange("p j h -> p (j h)")
            act(e1[:], flf, AF.Exp, [b_fl], [b_e1], scale=-1.0)
            act(l1[:], e1[:], AF.Ln, [b_e1], [b_l1], bias=1.0)
            cwf = cw[:].rearrange("p j h -> p (j h)"); saf = sa[:].rearrange("p j h -> p (j h)")
            for c0 in range(0, NF, 512):
                c1 = min(NF, c0 + 512)
                bk = nbank()
                mm(PS[bk][:, 0:c1 - c0], tri_f, l1[:, c0:c1], True, True, [b_const, b_l1], [PB[bk]])
                ve("dve", "tensor_copy", [PB[bk]], [b_cw], cwf[:, c0:c1], PS[bk][:, 0:c1 - c0])
                bk = nbank()
                mm(PS[bk][:, 0:c1 - c0], ones_f, l1[:, c0:c1], True, True, [b_const, b_l1], [PB[bk]])
                ve("dve", "tensor_copy", [PB[bk]], [b_sa], saf[:, c0:c1], PS[bk][:, 0:c1 - c0])
            a, b_, ba, bb = sa, sbb, b_sa, b_sb
            d = 1
            while d < NT:
                ve("dve", "tensor_tensor", [ba], [bb], b_[:, d:, :], a[:, d:, :], a[:, :NT - d, :], ALU.add)
                ve("dve", "tensor_copy", [ba], [bb], b_[:, :d, :], a[:, :d, :])
                a, b_, ba, bb = b_, a, bb, ba
                d *= 2
            ve("dve", "memset", [], [b_offs], offs[:, 0:1, :], 0.0)
            ve("dve", "tensor_copy", [ba], [b_offs], offs[:, 1:, :], a[:, :NT - 1, :])
            ve("dve", "tensor_tensor", [b_cw, b_offs], [b_NC], NCt[:], cw[:], offs[:], ALU.add)
        P.barrier()

        xT = sb("xT", [128, KC, 512], F32); b_xT = P.bufs(KC, "xT")
        xnT2 = sb("xnT2", [128, KC, 512], BF16); b_xn2 = P.bufs(KC, "xn2")
        big = sb("big", [128, 32, 512], BF16); b_big = P.bufs(32, "big")
        maskt = sb("maskt", [128, 16, 513], BF16); b_mask = P.buf("mask")
        stage = [sb(f"stage{i}", [128, D], F32) for i in range(2)]; b_stage = P.bufs(2, "stage")
        NW = 4
        wbuf = [sb(f"wbuf{i}", [128, KC, GW], BF16) for i in range(NW)]; b_wbuf = P.bufs(NW, "wbuf")
        NKV = 4
        kbuf = [sb(f"kbuf{i}", [128, 512], BF16) for i in range(NKV)]; b_kbuf = P.bufs(NKV, "kbuf")
        vbuf = [sb(f"vbuf{i}", [128, 4, 128], BF16) for i in range(NKV)]; b_vbuf = P.bufs(NKV, "vbuf")
        NR = 3
        PT = [sb(f"PT{i}", [128, 512], BF16) for i in range(NR)]; b_PT = P.bufs(NR, "PT")
        AT = [sb(f"AT{i}", [128, 1024], BF16) for i in range(2)]; b_AT = P.bufs(2, "AT")
        SPb = [sb(f"SPb{i}", [128, 1024], BF16) for i in range(2)]; b_SPb = P.bufs(2, "SPb")
        E1 = sb("E1", [128, 1024], F32); b_E1 = P.buf("E1")
        SPa = [sb(f"SPa{i}", [128, 512], BF16) for i in range(2)]; b_SPa = P.bufs(2, "SPa")
        biasT = [sb(f"biasT{i}", [128, NT], F32) for i in range(2)]; b_biasT = P.bufs(2, "biasT")
        crowb = [sb(f"crowb{i}", [16, 512], BF16) for i in range(2)]; b_crowb = P.bufs(2, "crowb")
        sq = [sb(f"sq{i}", [128, 512], BF16) for i in range(4)]; b_sq = P.bufs(4, "sq")
        rrep = sb("rrep", [128, 512], F32); b_rrep = P.buf("rrep")
        Lacc = sb("Lacc", [128, 512], F32); b_Lacc = P.buf("Lacc")
        tmpf = [sb(f"tmpf{i}", [128, 512], F32) for i in range(4)]; b_tmpf = P.bufs(4, "tmpf")

        P.dma("sp", maskt[:], masks_d, key="mask", writes=[b_mask])

        wctr = [0]

        def wload(src_ap, kcx):
            i = wctr[0] % NW
            wctr[0] += 1
            P.dma("sp", wbuf[i][:, 0:kcx, :], src_ap, key=f"w{i}", reads=[b_wscr], writes=[b_wbuf[i]])
            return wbuf[i], b_wbuf[i]

        def rmsnorm_fm(g_col, dst, b_dst, in_place=False):
            bk = nbank()
            for kc in range(KC):
                i = kc % 4
                act(sq[i][:], xT[:, kc, :], AF.Square, [b_xT[kc]], [b_sq[i]])
                mm(PS[bk][:, :], ones_bf, sq[i][:], kc == 0, kc == KC - 1, [b_const, b_sq[i]], [PB[bk]])
            ve("dve", "tensor_scalar", [PB[bk]], [b_rrep], rrep[:], PS[bk][:, :], 1.0 / D, RMS_EPS, ALU.mult, ALU.add)
            act(rrep[:], rrep[:], AF.Sqrt, [b_rrep], [b_rrep])
            ve("dve", "reciprocal", [b_rrep], [b_rrep], rrep[:], rrep[:])
            for kc in range(KC):
                ve("dve", "scalar_tensor_tensor", [b_xT[kc], b_rrep, b_const], [b_dst[kc]],
                   dst[:, kc, :], xT[:, kc, :], g_col[:, kc:kc + 1], rrep[:], ALU.mult, ALU.mult)

        kvctr = [0]
        tctr = [0]

        for m in range(NSLOT):
            for sub in range(4):
                i = tctr[0] % 2; tctr[0] += 1
                r0 = m * 512 + sub * 128
                P.dma("sp", stage[i][:], x_own[r0:r0 + 128, :], key=f"stage{i}", writes=[b_stage[i]])
                for k0 in range(0, KC, 4):
                    bk = nbank()
                    kn = min(4, KC - k0)
                    for q in range(kn):
                        tr(PS[bk][:, q * 128:(q + 1) * 128], stage[i][:, (k0 + q) * 128:(k0 + q + 1) * 128],
                           ident_f, [b_stage[i], b_const], [PB[bk]])
                    ve("dve", "tensor_copy", [PB[bk]], b_xT[k0:k0 + kn],
                       xT[:, k0:k0 + kn, sub * 128:(sub + 1) * 128],
                       PS[bk][:, 0:kn * 128].rearrange("p (k t) -> p k t", t=128))
            rmsnorm_fm(gmix, xnT2, b_xn2)
            for g in range(NGQ):
                wb, bwb = wload(wq_s[g], KC)
                for hh in range(GW // 128):
                    hd = g * (GW // 128) + hh
                    bk = nbank()
                    for kc in range(KC):
                        mm(PS[bk][:, :], wb[:, kc, hh * 128:(hh + 1) * 128], xnT2[:, kc, :], kc == 0, kc == KC - 1,
                           [bwb, b_xn2[kc]], [PB[bk]])
                    act(big[:, hd, :], PS[bk][:, :], AF.Copy, [PB[bk]], [b_big[hd]], scale=SCALE)
            nkt = 16 * m + 16
            FS = (0, 1); FO = 2; SO = 3; FL = 0; MISC = 0
            def make_head(h):
                hp = h % 2
                ve("dve", "tensor_scalar", [b_NC, b_offs], [b_biasT[hp]], biasT[hp][:, 0:nkt], NCt[:, 0:nkt, h],
                   offs[:, 16 * m, h:h + 1], None, ALU.subtract)
                tr(PS[MISC][0:16, 0:128], NCt[:, 16 * m:16 * m + 16, h], ident_f, [b_NC, b_const], [PB[MISC]])
                for i4 in range(4):
                    ve("dve", "tensor_scalar", [PB[MISC], b_offs, b_const], [b_crowb[hp]],
                       crowb[hp][:, i4 * 128:(i4 + 1) * 128], PS[MISC][0:16, 0:128],
                       offs[0:16, 16 * m, h:h + 1], negselc[:, i4:i4 + 1], ALU.subtract, ALU.mult)
                qf = big[:, h, :]; qs = big[:, H + h, :]
                kv = {}

                def load_kv(mx, kb):
                    if (mx, kb) in kv:
                        return kv[(mx, kb)]
                    i = kvctr[0] % NKV; kvctr[0] += 1
                    P.dma("sp", kbuf[i][:], KT_s[mx, h, :, kb * 512:(kb + 1) * 512], key=f"kb{i}",
                          reads=[b_KV], writes=[b_kbuf[i]])
                    P.dma("sp", vbuf[i][:], V_s[mx, kb * 512:(kb + 1) * 512, h * 128:(h + 1) * 128]
                          .rearrange("(k p) d -> p k d", p=128), key=f"vb{i}", reads=[b_KV], writes=[b_vbuf[i]])
                    kv.clear()
                    kv[(mx, kb)] = i
                    return i

                npair = nkt // 2
                fst = {}
                sst = {}
                cur = {}

                def fox_A(t):
                    if t % 4 == 0:
                        cur["f"] = load_kv(0, t // 4)
                    ikv = cur["f"]
                    bS = FS[t % 2]
                    dg = t >= 16 * m
                    mm(PS[bS][:, :], kbuf[ikv][:, (t % 4) * 128:(t % 4 + 1) * 128], qf, True, False,
                       [b_kbuf[ikv], b_big[h]], [PB[bS]])
                    mm(PS[bS][:, :], ones_bf[0:16, :], crowb[hp][:, :], False, not dg,
                       [b_const, b_crowb[hp]], [PB[bS]])
                    if dg:
                        mm(PS[bS][:, :], ident_bf, maskt[:, t - 16 * m, 1:513], False, True,
                           [b_const, b_mask], [PB[bS]])
                    ip = t % NR
                    act(PT[ip][:], PS[bS][:, :], AF.Exp, [PB[bS], b_biasT[hp]], [b_PT[ip]],
                        bias=biasT[hp][:, t:t + 1])
                    fst[t] = (ikv, t % 4, ip)

                def fox_B(t):
                    ikv, kk, ip = fst.pop(t)
                    mm(PS[FO][:, :], vbuf[ikv][:, kk, :], PT[ip][:], t == 0, t == nkt - 1,
                       [b_vbuf[ikv], b_PT[ip]], [PB[FO]])
                    if t == 0:
                        ve("dve", "tensor_copy", [b_PT[ip]], [b_Lacc], Lacc[:], PT[ip][:])
                    else:
                        ve("dve", "tensor_tensor", [b_PT[ip], b_Lacc], [b_Lacc], Lacc[:], Lacc[:], PT[ip][:],
                           ALU.add)

                def sb_A(ps):
                    JA = nkt - 1 - 2 * ps
                    if JA % 4 == 3:
                        cur["s"] = load_kv(1, JA // 4)
                    ikv = cur["s"]
                    z = ps % 2
                    for half, J in ((0, JA), (1, JA - 1)):
                        bZ = 4 + 2 * z + half
                        dg = J >= 16 * m
                        mm(PS[bZ][:, :], kbuf[ikv][:, (J % 4) * 128:(J % 4 + 1) * 128], qs, True, False,
                           [b_kbuf[ikv], b_big[H + h]], [PB[bZ]], skip_group_check=True)
                        if dg:
                            mm(PS[bZ][:, :], ident_bf, maskt[:, J - 16 * m, 0:512], False, False,
                               [b_const, b_mask], [PB[bZ]], skip_group_check=True)
                    sst[ps] = (ikv, JA % 4, z)

                def sb_A_act(ps):
                    ikv, kkA, z = sst[ps]
                    pbz = [PB[4 + 2 * z], PB[5 + 2 * z]]
                    act(E1[:], PSZ[z][:, :], AF.Exp, pbz, [b_E1])
                    act(SPb[z][:], E1[:], AF.Ln, [b_E1], [b_SPb[z]], bias=1.0)

                def sb_B(ps):
                    ikv, kkA, z = sst[ps]
                    bA = 4 + 2 * z; bB = 5 + 2 * z
                    prev = (ps - 1) % 2
                    mm(PS[bA][:, :], negtri, SPb[z][:, 0:512], False, ps == 0, [b_const, b_SPb[z]], [PB[bA]],
                       skip_group_check=True)
                    if ps >= 1:
                        mm(PS[bA][:, :], negones, SPa[prev][:], False, True, [b_const, b_SPa[prev]], [PB[bA]],
                           skip_group_check=True)
                    mm(PS[bB][:, :], negtri, SPb[z][:, 512:1024], False, False, [b_const, b_SPb[z]], [PB[bB]],
                       skip_group_check=True)
                    mm(PS[bB][:, :], negones, SPb[z][:, 0:512], False, ps == 0, [b_const, b_SPb[z]], [PB[bB]],
                       skip_group_check=True)
                    if ps >= 1:
                        mm(PS[bB][:, :], negones, SPa[prev][:], False, True, [b_const, b_SPa[prev]], [PB[bB]],
                           skip_group_check=True)
                    act(AT[z][:], PSZ[z][:, :], AF.Exp, [PB[bA], PB[bB]], [b_AT[z]])
                    if ps < npair - 1:
                        if ps == 0:
                            ve("pool", "tensor_tensor", [b_SPb[z]], [b_SPa[0]], SPa[0][:], SPb[z][:, 0:512],
                               SPb[z][:, 512:1024], ALU.add)
                        else:
                            ve("pool", "tensor_tensor", [b_SPb[z], b_SPa[prev]], [b_SPa[ps % 2]],
                               SPa[ps % 2][:], SPa[prev][:], SPb[z][:, 0:512], ALU.add)
                            ve("pool", "tensor_tensor", [b_SPb[z], b_SPa[ps % 2]], [b_SPa[ps % 2]],
                               SPa[ps % 2][:], SPa[ps % 2][:], SPb[z][:, 512:1024], ALU.add)

                def sb_C(ps):
                    ikv, kkA, z = sst.pop(ps)
                    mm(PS[SO][:, :], vbuf[ikv][:, kkA, :], AT[z][:, 0:512], ps == 0, False,
                       [b_vbuf[ikv], b_AT[z]], [PB[SO]])
                    mm(PS[SO][:, :], vbuf[ikv][:, kkA - 1, :], AT[z][:, 512:1024], False, ps == npair - 1,
                       [b_vbuf[ikv], b_AT[z]], [PB[SO]])

                def step(ps):
                    if ps < npair:
                        fox_A(2 * ps)
                        fox_A(2 * ps + 1)
                    if 1 <= ps <= npair:
                        sb_B(ps - 1)
                    if ps < npair:
                        sb_A(ps)
                        sb_A_act(ps)
                    if 1 <= ps <= npair:
                        fox_B(2 * ps - 1)
                    if ps < npair:
                        fox_B(2 * ps)
                    if ps >= 2:
                        sb_C(ps - 2)
                def fin_f():
                    i = h % 4
                    mm(PS[FL][:, :], ones_f, Lacc[:], True, True, [b_const, b_Lacc], [PB[FL]])
                    ve("dve", "reciprocal", [PB[FL]], [b_tmpf[i]], tmpf[i][:], PS[FL][:, :])
                    ve("dve", "tensor_tensor", [PB[FO], b_tmpf[i]], [b_big[2 * H + h]], big[:, 2 * H + h, :],
                       PS[FO][:, :], tmpf[i][:], ALU.mult)

                def fin_s():
                    ve("dve", "tensor_copy", [PB[SO]], [b_big[3 * H + h]], big[:, 3 * H + h, :], PS[SO][:, :])

                return step, fin_f, fin_s

            npair_m = nkt // 2
            prev = None
            for h in range(H):
                hd = make_head(h)
                for ps in range(npair_m):
                    if prev is not None and ps < 2:
                        prev[0](npair_m + ps)
                        if ps == 0:
                            prev[1]()
                        else:
                            prev[2]()
                    hd[0](ps)
                prev = hd
            prev[0](npair_m); prev[1](); prev[0](npair_m + 1); prev[2]()
            for g in range(NGD):
                wA, bA = wload(wof_s[g], H)
                wB, bB = wload(wos_s[g], H)
                wGa, bGa = wload(wga_s[g], KC)
                wGb, bGb = wload(wgb_s[g], KC)
                for cc in range(GW // 128):
                    c = g * (GW // 128) + cc
                    cs = slice(cc * 128, (cc + 1) * 128)
                    byA = nbank(); byB = nbank(); bgA = nbank(); bgB = nbank()
                    for h in range(H):
                        mm(PS[byA][:, :], wA[:, h, cs], big[:, 2 * H + h, :], h == 0, h == H - 1,
                           [bA, b_big[2 * H + h]], [PB[byA]])
                    for h in range(H):
                        mm(PS[byB][:, :], wB[:, h, cs], big[:, 3 * H + h, :], h == 0, h == H - 1,
                           [bB, b_big[3 * H + h]], [PB[byB]])
                    for kc in range(KC):
                        mm(PS[bgA][:, :], wGa[:, kc, cs], xnT2[:, kc, :], kc == 0, kc == KC - 1,
                           [bGa, b_xn2[kc]], [PB[bgA]])
                    for kc in range(KC):
                        mm(PS[bgB][:, :], wGb[:, kc, cs], xnT2[:, kc, :], kc == 0, kc == KC - 1,
                           [bGb, b_xn2[kc]], [PB[bgB]])
                    act(tmpf[0][:], PS[bgA][:, :], AF.Sigmoid, [PB[bgA]], [b_tmpf[0]])
                    act(tmpf[1][:], PS[bgB][:, :], AF.Sigmoid, [PB[bgB]], [b_tmpf[1]])
                    ve("dve", "tensor_tensor", [PB[byA], b_tmpf[0]], [b_tmpf[2]], tmpf[2][:], PS[byA][:, :],
                       tmpf[0][:], ALU.mult)
                    ve("dve", "tensor_tensor", [PB[byB], b_tmpf[1]], [b_tmpf[3]], tmpf[3][:], PS[byB][:, :],
                       tmpf[1][:], ALU.mult)
                    ve("dve", "tensor_tensor", [b_tmpf[2], b_tmpf[3]], [b_big[c]], big[:, c, :], tmpf[2][:],
                       tmpf[3][:], ALU.add)
            for g in range(NGD):
                wb, bwb = wload(wo_s[g], KC)
                for cc in range(GW // 128):
                    c = g * (GW // 128) + cc
                    bk = nbank()
                    for kc in range(KC):
                        mm(PS[bk][:, :], wb[:, kc, cc * 128:(cc + 1) * 128], big[:, kc, :], kc == 0, kc == KC - 1,
                           [bwb, b_big[kc]], [PB[bk]])
                    ve("dve", "tensor_tensor", [PB[bk], b_xT[c]], [b_xT[c]], xT[:, c, :], xT[:, c, :], PS[bk][:, :],
                       ALU.add)
            rmsnorm_fm(gmlp, xnT2, b_xn2)
            for half in range(2):
                for g in range(HJ * 128 // GW):
                    wb, bwb = wload(wup_s[half * (HJ * 128 // GW) + g], KC)
                    for jj in range(GW // 128):
                        j = g * (GW // 128) + jj
                        bk = nbank()
                        for kc in range(KC):
                            mm(PS[bk][:, :], wb[:, kc, jj * 128:(jj + 1) * 128], xnT2[:, kc, :], kc == 0,
                               kc == KC - 1, [bwb, b_xn2[kc]], [PB[bk]])
                        i = j % 4
                        act(tmpf[i][:], PS[bk][:, :], AF.Square, [PB[bk]], [b_tmpf[i]])
                        ve("dve", "scalar_tensor_tensor", [PB[bk], b_tmpf[i]], [b_big[j]], big[:, j, :],
                           PS[bk][:, :], 0.0, tmpf[i][:], ALU.is_gt, ALU.mult)
                for g in range(NGD):
                    pieces = []
                    step_k = min(KC, HJ)
                    for k0 in range(0, HJ, step_k):
                        wb, bwb = wload(wdn_s[g, :, half * HJ + k0: half * HJ + k0 + step_k, :], step_k)
                        pieces.append((wb, bwb, k0, step_k))
                    for cc in range(GW // 128):
                        c = g * (GW // 128) + cc
                        bk = nbank()
                        n = 0
                        for (wb, bwb, k0, sk) in pieces:
                            for kk in range(sk):
                                mm(PS[bk][:, :], wb[:, kk, cc * 128:(cc + 1) * 128], big[:, k0 + kk, :], n == 0,
                                   n == HJ - 1, [bwb, b_big[k0 + kk]], [PB[bk]])
                                n += 1
                        ve("dve", "tensor_tensor", [PB[bk], b_xT[c]], [b_xT[c]], xT[:, c, :], xT[:, c, :],
                           PS[bk][:, :], ALU.add)
            rmsnorm_fm(gfin, xT, b_xT)
            for sub in range(4):
                i = tctr[0] % 2; tctr[0] += 1
                for k0 in range(0, KC, 4):
                    bk = nbank()
                    kn = min(4, KC - k0)
                    for q in range(kn):
                        tr(PS[bk][:, q * 128:(q + 1) * 128], xT[:, k0 + q, sub * 128:(sub + 1) * 128], ident_f,
                           [b_xT[k0 + q], b_const], [PB[bk]])
                    act(stage[i][:, k0 * 128:(k0 + kn) * 128], PS[bk][:, 0:kn * 128], AF.Copy, [PB[bk]],
                        [b_stage[i]])
                r0 = m * 512 + sub * 128
                P.dma("pool", out_d[r0:r0 + 128, :], stage[i][:], key=f"out{i}", reads=[b_stage[i]],
                      writes=[P.buf()])
        P.barrier()
        P.emit()
    return nc, P


def host_consts(r):
    cbf = np.zeros((128, 4, 128), np.float32)
    cbf[:, 0, :] = np.eye(128)
    j = np.arange(128)[:, None]; s = np.arange(128)[None, :]
    cbf[:, 1, :] = -(j >= s).astype(np.float32)
    cbf[:, 2, :] = -1.0
    cbf[:, 3, :] = 1.0
    cf32 = np.zeros((128, 3, 128), np.float32)
    cf32[:, 0, :] = np.eye(128)
    cf32[:, 1, :] = (j <= s).astype(np.float32)
    cf32[:, 2, :] = 1.0
    sel = np.zeros((16, 4), np.float32)
    for i in range(4):
        sel[4 * r + i, i] = -1.0
    sidx = np.arange(128)[:, None, None]
    idx = np.arange(16)[None, :, None]
    col = np.arange(513)[None, None, :]
    keypos = 128 * idx + sidx
    qpos = 512 * r + col - 1
    masks = np.where(keypos > qpos, NEG, 0.0).astype(np.float32)
    bf = ml_dtypes.bfloat16
    return cbf.astype(bf), cf32, sel, masks.astype(bf)


_CACHE = {}


def run(cfg, x, norm_mix_g, w_in, b_forget, w_out_fox, w_out_sb, w_out, norm_mlp_g, w_mlp_up, w_mlp_down,
        norm_final_g):
    D = cfg["D"]; H = cfg["H"]; NSLOT = cfg["NSLOT"]; KC = D // 128
    S = 2048 * NSLOT
    key = tuple(sorted(cfg.items()))
    if key not in _CACHE:
        _CACHE[key] = build(cfg)
    nc, P = _CACHE[key]
    f = lambda a: np.ascontiguousarray(np.asarray(a, dtype=np.float32))
    x = f(x)

    def col(g):
        return np.ascontiguousarray(f(g).reshape(KC, 128).T)

    in_maps = []
    for c in range(8):
        b = c // 4; r = c % 4
        cbf, cf32, sel, masks = host_consts(r)
        xb = x[b]
        own = np.ascontiguousarray(xb.reshape(NSLOT, 4, 512, D)[:, r].reshape(NSLOT * 512, D))
        in_maps.append(dict(
            x_seq=xb, x_own=own, w_in=f(w_in[0]), w_of=f(w_out_fox[0]), w_os=f(w_out_sb[0]), w_o=f(w_out[0]),
            w_up=f(w_mlp_up[0]), w_dn=f(w_mlp_down[0]), gmix=col(norm_mix_g[0]), gmlp=col(norm_mlp_g[0]),
            gfin=col(norm_final_g), bfg=np.ascontiguousarray(np.broadcast_to(f(b_forget[0])[None, :], (128, H))),
            masks=masks, sel=sel, cbf=cbf, cf32=cf32))
    res = run_bass_kernel_spmd(nc, in_maps, core_ids=list(range(8)))
    out = np.empty((2, S, D), np.float32)
    for c in range(8):
        b = c // 4; r = c % 4
        o = np.asarray(res.results[c]["out"]).reshape(NSLOT, 512, D)
        out[b].reshape(NSLOT, 4, 512, D)[:, r] = o
    return out


def kernel(**inputs):
    return run(FULL, **inputs)
```

```python
import contextlib
import numpy as np
import ml_dtypes
import concourse.bass as bass
import concourse.mybir as mybir
from concourse.bass_utils import run_bass_kernel_spmd

F32 = mybir.dt.float32
BF16 = mybir.dt.bfloat16
AF = mybir.ActivationFunctionType
ALU = mybir.AluOpType

ENGS = ("pe", "act", "dve", "pool", "sp")
EPOCH = 20000
DMA_EPOCH = 1500
NEG = -30000.0
RMS_EPS = 1e-6


class Buf:
    __slots__ = ("name", "writers", "readers")

    def __init__(self, name):
        self.name = name
        self.writers = []
        self.readers = []


class Op:
    __slots__ = ("eng", "fn", "is_dma", "key", "deps", "flag", "cnt", "sem", "barrier")

    def __init__(self, eng, fn, is_dma=False, key=None):
        self.eng = eng
        self.fn = fn
        self.is_dma = is_dma
        self.key = key
        self.deps = []
        self.flag = False
        self.cnt = None
        self.sem = None
        self.barrier = None


class Prog:
    def __init__(self, nc):
        self.nc = nc
        self.ops = {e: [] for e in ENGS}
        self.dma_keys = {}
        self.nbuf = 0
        self.n_ops = 0

    def buf(self, name=None):
        self.nbuf += 1
        return Buf(name or f"b{self.nbuf}")

    def bufs(self, n, name="b"):
        return [self.buf(f"{name}{i}") for i in range(n)]

    def _track(self, op, reads, writes):
        deps = []
        for b in reads:
            deps.extend(b.writers)
            b.readers.append(op)
        for b in writes:
            if b.readers:
                deps.extend(b.readers)
                deps.extend(b.writers)
                b.writers = [op]
                b.readers = []
            elif b.writers and not all(w.eng == op.eng for w in b.writers):
                deps.extend(b.writers)
                b.writers = [op]
            elif op.is_dma or any(w.is_dma for w in b.writers):
                b.writers.append(op)
            else:
                b.writers = [op]
        out = []
        seen = set()
        for d in deps:
            if d is op or id(d) in seen:
                continue
            seen.add(id(d))
            out.append(d)
        return out

    def op(self, eng, fn, reads=(), writes=()):
        o = Op(eng, fn)
        raw = set()
        for b in reads:
            for w in b.writers:
                raw.add(id(w))
        deps = self._track(o, reads, writes)
        keep = []
        for d in deps:
            if (not d.is_dma) and d.eng == eng:
                if eng == "pe" or id(d) not in raw:
                    continue
            keep.append(d)
            d.flag = True
        o.deps = keep
        self.ops[eng].append(o)
        self.n_ops += 1
        return o

    def dma(self, eng, out, in_, key, reads=(), writes=(), **kw):
        def fn(e):
            return e.dma_start(out=out, in_=in_, **kw)
        o = Op(eng, fn, is_dma=True, key=key)
        o.deps = self._track(o, reads, writes)
        for d in o.deps:
            d.flag = True
        self.ops[eng].append(o)
        self.dma_keys.setdefault(key, []).append(o)
        self.n_ops += 1
        return o

    def barrier(self):
        targets = []
        for e in ENGS:
            for o in reversed(self.ops[e]):
                if (not o.is_dma) and o.barrier is None:
                    o.flag = True
                    targets.append(o)
                    break
        for lst in self.dma_keys.values():
            if lst:
                targets.append(lst[-1])
        for e in ENGS:
            o = Op(e, None)
            o.barrier = list(targets)
            self.ops[e].append(o)

    def emit(self):
        nc = self.nc
        sems = {}

        def get_sem(name):
            if name not in sems:
                sems[name] = nc.alloc_semaphore(name=name)
            return sems[name]

        for e in ENGS:
            c = 0
            ep = 0
            for o in self.ops[e]:
                if o.is_dma or o.barrier is not None:
                    continue
                if o.flag:
                    c += 1
                    if c > EPOCH:
                        ep += 1
                        c = 1
                    o.sem = f"c_{e}_{ep}"
                    o.cnt = c
        for key, lst in self.dma_keys.items():
            c = 0
            ep = 0
            for o in lst:
                c += 1
                if c > DMA_EPOCH:
                    ep += 1
                    c = 1
                o.sem = f"d_{key}_{ep}"
                o.cnt = 16 * c

        def emit_engine(ename, e):
            waited = {}
            for o in self.ops[ename]:
                deps = o.barrier if o.barrier is not None else o.deps
                need = {}
                for d in deps:
                    if d.sem is None:
                        continue
                    if need.get(d.sem, 0) < d.cnt:
                        need[d.sem] = d.cnt
                for s, v in need.items():
                    if waited.get(s, 0) >= v:
                        continue
                    e.wait_ge(get_sem(s), v)
                    waited[s] = v
                if o.barrier is not None:
                    continue
                inst = o.fn(e)
                if o.is_dma:
                    inst.then_inc(get_sem(o.sem), 16)
                elif o.flag:
                    inst.then_inc(get_sem(o.sem), 1)

        with nc.Block() as block:
            @block.tensor
            def _(e):
                emit_engine("pe", e)

            @block.scalar
            def _(e):
                emit_engine("act", e)

            @block.vector
            def _(e):
                emit_engine("dve", e)

            @block.gpsimd
            def _(e):
                emit_engine("pool", e)

            @block.sync
            def _(e):
                emit_engine("sp", e)
        self.n_sems = len(sems)


FULL = dict(D=2048, H=8, DFF=8192, NSLOT=8)


def build(cfg):
    D = cfg["D"]; H = cfg["H"]; DFF = cfg["DFF"]; NSLOT = cfg["NSLOT"]
    KC = D // 128; W = H * 128; JC = DFF // 128
    S = 2048 * NSLOT; NB = S // 512; NT = S // 128; NOWN = 512 * NSLOT
    GW = 256
    HJ = JC // 2
    assert HJ <= 32 and HJ % KC == 0 or HJ <= KC
    IN_COLS = 6 * W + H + 2 * D
    oqf = 0; okf = W; ovf = 2 * W; off_ = 3 * W
    oqs = 3 * W + H; oks = oqs + W; ovs = oks + W; oga = ovs + W; ogb = oga + D
    SCALE = 128 ** -0.5

    nc = bass.Bass("TRN2", target_bir_lowering=False)
    P = Prog(nc)

    def din(name, shape, dt=F32):
        return nc.dram_tensor(name, list(shape), dt, kind="ExternalInput").ap()

    def dscr(name, shape, dt=BF16):
        return nc.dram_tensor(name, list(shape), dt, kind="Internal").ap()

    x_seq = din("x_seq", [S, D]); x_own = din("x_own", [NOWN, D])
    w_in = din("w_in", [D, IN_COLS]); w_of = din("w_of", [W, D]); w_os = din("w_os", [W, D])
    w_o = din("w_o", [D, D]); w_up = din("w_up", [D, DFF]); w_dn = din("w_dn", [DFF, D])
    gmix_d = din("gmix", [128, KC]); gmlp_d = din("gmlp", [128, KC]); gfin_d = din("gfin", [128, KC])
    bfg_d = din("bfg", [128, H])
    masks_d = din("masks", [128, 16, 513], BF16)
    sel_d = din("sel", [16, 4], F32)
    cbf_d = din("cbf", [128, 4, 128], BF16)
    cf32_d = din("cf32", [128, 3, 128], F32)
    out_d = nc.dram_tensor("out", [NOWN, D], F32, kind="ExternalOutput").ap()

    NGQ = 2 * W // GW; NGD = D // GW; NGU = DFF // GW
    wq_s = dscr("wq_s", [NGQ, 128, KC, GW]); wga_s = dscr("wga_s", [NGD, 128, KC, GW])
    wgb_s = dscr("wgb_s", [NGD, 128, KC, GW]); wof_s = dscr("wof_s", [NGD, 128, H, GW])
    wos_s = dscr("wos_s", [NGD, 128, H, GW]); wo_s = dscr("wo_s", [NGD, 128, KC, GW])
    wup_s = dscr("wup_s", [NGU, 128, KC, GW]); wdn_s = dscr("wdn_s", [NGD, 128, JC, GW])
    KT_s = dscr("KT_s", [2, H, 128, S]); V_s = dscr("V_s", [2, S, W])

    es = contextlib.ExitStack()

    def sb(name, shape, dt, stack=None):
        return (stack or es).enter_context(nc.sbuf_tensor("s_" + name, list(shape), dt))

    with es:
        PSZ = [es.enter_context(nc.psum_tensor(f"psz{i}", [128, 1024], F32)) for i in range(2)]
        PSa = [es.enter_context(nc.psum_tensor(f"ps{i}", [128, 512], F32)) for i in range(4)]
        PS = [p[:, :] for p in PSa] + [PSZ[0][:, 0:512], PSZ[0][:, 512:1024], PSZ[1][:, 0:512], PSZ[1][:, 512:1024]]
        PB = P.bufs(8, "psb")
        rr = [0]

        def nbank():
            i = rr[0] % 8
            rr[0] += 1
            return i

        cbf = sb("cbf", [128, 4, 128], BF16); cf32 = sb("cf32", [128, 3, 128], F32)
        negselc = sb("negselc", [16, 4], F32)
        gmix = sb("gmix", [128, KC], F32); gmlp = sb("gmlp", [128, KC], F32); gfin = sb("gfin", [128, KC], F32)
        bfg = sb("bfg", [128, H], F32)
        NCt = sb("NCt", [128, NT, H], F32); offs = sb("offs", [128, NT, H], F32)
        fl = sb("fl", [128, NT, H], F32)
        b_const = P.buf("const"); b_NC = P.buf("NC"); b_offs = P.buf("offs"); b_fl = P.buf("fl")
        ident_bf = cbf[:, 0, :]; negtri = cbf[:, 1, :]; negones = cbf[:, 2, :]; ones_bf = cbf[:, 3, :]
        ident_f = cf32[:, 0, :]; tri_f = cf32[:, 1, :]; ones_f = cf32[:, 2, :]

        for dst, src in ((cbf, cbf_d), (cf32, cf32_d), (negselc, sel_d), (gmix, gmix_d), (gmlp, gmlp_d),
                         (gfin, gfin_d), (bfg, bfg_d)):
            P.dma("sp", dst[:], src, key="const", writes=[b_const])

        def mm(out, lhsT, rhs, start, stop, reads, writes, **kw):
            P.op("pe", lambda e: e.matmul(out, lhsT, rhs, start=start, stop=stop, **kw), reads, writes)

        def tr(out, in_, ident, reads, writes):
            P.op("pe", lambda e: e.transpose(out, in_, ident), reads, writes)

        def act(out, in_, func, reads, writes, **kw):
            P.op("act", lambda e: e.activation(out, in_, func, **kw), reads, writes)

        def ve(eng, name, reads, writes, *a, **kw):
            P.op(eng, lambda e: getattr(e, name)(*a, **kw), reads, writes)

        b_wscr = P.buf("wscr")

        def cast_groups(w_ap, r0, nrows, c0, ncols, dst, gbase):
            kcx = nrows // 128
            step = 16 if kcx > 16 else kcx
            for g in range(ncols // GW):
                for k0 in range(0, kcx, step):
                    src = w_ap[r0 + k0 * 128: r0 + (k0 + step) * 128, c0 + g * GW: c0 + (g + 1) * GW]
                    src = src.rearrange("(kc f) c -> f kc c", f=128)
                    P.dma("pool", dst[gbase + g, :, k0:k0 + step, :], src, key="cast", writes=[b_wscr])

        with contextlib.ExitStack() as s1:
            wkv = sb("wkv", [128, KC, 2 * W + 8], BF16, s1)
            xst = [sb(f"xst{i}", [128, D], F32, s1) for i in range(4)]
            xs = [sb(f"xs{i}", [128, D], BF16, s1) for i in range(4)]
            junk = sb("junk", [128, D], BF16, s1)
            ssq = sb("ssq", [128, 8], F32, s1); rstd = sb("rstd", [128, 8], F32, s1)
            xnT = [sb(f"xnT{i}", [128, KC, 512], BF16, s1) for i in range(2)]
            kst = [sb(f"kst{i}", [128, H, 512], BF16, s1) for i in range(2)]
            vst = [sb(f"vst{i}", [128, 4, W], BF16, s1) for i in range(2)]
            b_wkv = P.buf("wkv"); b_xst = P.bufs(4, "xst"); b_xs = P.bufs(4, "xs"); b_junk = P.buf("junk")
            b_ssq = P.bufs(8, "ssq"); b_rstd = P.bufs(8, "rstd"); b_xnT = P.bufs(2, "xnT")
            b_kst = P.bufs(2, "kst"); b_vst = P.bufs(2, "vst"); b_KV = P.buf("KVs")

            cnt = [0]

            def load_wkv(mx):
                ok_, ov_ = (okf, ovf) if mx == 0 else (oks, ovs)
                for (c0, dcol, n) in ((ok_, 0, W), (ov_, W, W)) + (((off_, 2 * W, H),) if mx == 0 else ()):
                    for k0 in range(0, KC, 4):
                        k1 = min(KC, k0 + 4)
                        src = w_in[k0 * 128:k1 * 128, c0:c0 + n].rearrange("(kc f) c -> f kc c", f=128)
                        P.dma("pool", wkv[:, k0:k1, dcol:dcol + n], src, key="wkv", writes=[b_wkv])

            def pre(gb):
                blk = gb % NB
                for sub in range(4):
                    i = cnt[0] % 4; j = cnt[0] % 8; cnt[0] += 1
                    r0 = blk * 512 + sub * 128
                    P.dma("sp", xst[i][:], x_seq[r0:r0 + 128, :], key=f"xst{i}", writes=[b_xst[i]])
                    act(junk[:], xst[i][:], AF.Square, [b_xst[i]], [b_junk, b_ssq[j]],
                        accum_out=ssq[:, j:j + 1])
                    ve("dve", "tensor_scalar", [b_ssq[j]], [b_rstd[j]], rstd[:, j:j + 1], ssq[:, j:j + 1],
                       1.0 / D, RMS_EPS, ALU.mult, ALU.add)
                    act(rstd[:, j:j + 1], rstd[:, j:j + 1], AF.Sqrt, [b_rstd[j]], [b_rstd[j]])
                    ve("dve", "reciprocal", [b_rstd[j]], [b_rstd[j]], rstd[:, j:j + 1], rstd[:, j:j + 1])
                    ve("dve", "tensor_scalar", [b_xst[i], b_rstd[j]], [b_xs[sub]], xs[sub][:], xst[i][:],
                       rstd[:, j:j + 1], None, ALU.mult)

            def trans(gb):
                xn = xnT[gb % 2]; bxn = b_xnT[gb % 2]
                for kc in range(KC):
                    bk = nbank()
                    for sub in range(4):
                        mm(PS[bk][:, sub * 128:(sub + 1) * 128], xs[sub][:, kc * 128:(kc + 1) * 128],
                           ident_bf, True, True, [b_xs[sub], b_const], [PB[bk]], skip_group_check=True)
                    act(xn[:, kc, :], PS[bk][:, 0:512], AF.Copy, [PB[bk], b_const], [bxn],
                        scale=gmix[:, kc:kc + 1])

            def kproj(gb):
                mx = gb // NB; blk = gb % NB
                xn = xnT[gb % 2]; bxn = b_xnT[gb % 2]
                ks = kst[gb % 2]; bks = b_kst[gb % 2]
                for h in range(H):
                    bk = nbank()
                    for kc in range(KC):
                        mm(PS[bk][:, :], wkv[:, kc, h * 128:(h + 1) * 128], xn[:, kc, :], kc == 0, kc == KC - 1,
                           [b_wkv, bxn], [PB[bk]])
                    ve("dve", "tensor_copy", [PB[bk]], [bks], ks[:, h, :], PS[bk][:, :])
                P.dma("pool", KT_s[mx, :, :, blk * 512:(blk + 1) * 512].rearrange("h d t -> d h t"), ks[:],
                      key=f"kst{gb % 2}", reads=[bks], writes=[b_KV])

            def vproj(gb):
                mx = gb // NB; blk = gb % NB
                xn = xnT[gb % 2]; bxn = b_xnT[gb % 2]
                vs = vst[gb % 2]; bvs = b_vst[gb % 2]
                for sub in range(4):
                    for cg in range(W // 512):
                        bk = nbank()
                        for kc in range(KC):
                            mm(PS[bk][:, :], xn[:, kc, sub * 128:(sub + 1) * 128],
                               wkv[:, kc, W + cg * 512: W + (cg + 1) * 512], kc == 0, kc == KC - 1,
                               [b_wkv, bxn], [PB[bk]])
                        if (sub + cg) % 2 == 0:
                            act(vs[:, sub, cg * 512:(cg + 1) * 512], PS[bk][:, :], AF.Copy, [PB[bk]], [bvs])
                        else:
                            ve("dve", "tensor_copy", [PB[bk]], [bvs], vs[:, sub, cg * 512:(cg + 1) * 512],
                               PS[bk][:, :])
                P.dma("pool", V_s[mx, blk * 512:(blk + 1) * 512, :].rearrange("(s p) c -> p s c", p=128),
                      vs[:], key=f"vst{gb % 2}", reads=[bvs], writes=[b_KV])
                if mx == 0:
                    for sub in range(4):
                        bk = nbank()
                        for kc in range(KC):
                            mm(PS[bk][:, 0:H], xn[:, kc, sub * 128:(sub + 1) * 128],
                               wkv[:, kc, 2 * W:2 * W + H], kc == 0, kc == KC - 1, [b_wkv, bxn], [PB[bk]])
                        ve("dve", "tensor_tensor", [PB[bk], b_const], [b_fl], fl[:, blk * 4 + sub, :],
                           PS[bk][:, 0:H], bfg[:, :], ALU.add)

            load_wkv(0)
            pre(0)
            for i in range(2):
                cast_groups(w_in, 0, D, (oqf, oqs)[i], W, wq_s, i * (W // GW))
            cast_groups(w_in, 0, D, oga, D, wga_s, 0)
            cast_groups(w_in, 0, D, ogb, D, wgb_s, 0)
            cast_groups(w_of, 0, W, 0, D, wof_s, 0)
            cast_groups(w_os, 0, W, 0, D, wos_s, 0)
            cast_groups(w_o, 0, D, 0, D, wo_s, 0)
            cast_groups(w_up, 0, D, 0, DFF, wup_s, 0)
            cast_groups(w_dn, 0, DFF, 0, D, wdn_s, 0)
            trans(0)
            for gb in range(2 * NB):
                if gb == NB:
                    load_wkv(1)
                if gb + 1 < 2 * NB:
                    pre(gb + 1)
                kproj(gb)
                if gb + 1 < 2 * NB:
                    trans(gb + 1)
                vproj(gb)
        P.barrier()

        NF = NT * H
        with contextlib.ExitStack() as s2:
            e1 = sb("e1", [128, NF], F32, s2); l1 = sb("l1", [128, NF], F32, s2)
            cw = sb("cw", [128, NT, H], F32, s2)
            sa = sb("sa", [128, NT, H], F32, s2); sbb = sb("sbb", [128, NT, H], F32, s2)
            b_e1 = P.buf(); b_l1 = P.buf(); b_cw = P.buf(); b_sa = P.buf(); b_sb = P.buf()
            flf = fl[:].rearrange("p j h -> p (j h)")
            act(e1[:], flf, AF.Exp, [b_fl], [b_e1], scale=-1.0)
            act(l1[:], e1[:], AF.Ln, [b_e1], [b_l1], bias=1.0)
            cwf = cw[:].rearrange("p j h -> p (j h)"); saf = sa[:].rearrange("p j h -> p (j h)")
            for c0 in range(0, NF, 512):
                c1 = min(NF, c0 + 512)
                bk = nbank()
                mm(PS[bk][:, 0:c1 - c0], tri_f, l1[:, c0:c1], True, True, [b_const, b_l1], [PB[bk]])
                ve("dve", "tensor_copy", [PB[bk]], [b_cw], cwf[:, c0:c1], PS[bk][:, 0:c1 - c0])
                bk = nbank()
                mm(PS[bk][:, 0:c1 - c0], ones_f, l1[:, c0:c1], True, True, [b_const, b_l1], [PB[bk]])
                ve("dve", "tensor_copy", [PB[bk]], [b_sa], saf[:, c0:c1], PS[bk][:, 0:c1 - c0])
            a, b_, ba, bb = sa, sbb, b_sa, b_sb
            d = 1
            while d < NT:
                ve("dve", "tensor_tensor", [ba], [bb], b_[:, d:, :], a[:, d:, :], a[:, :NT - d, :], ALU.add)
                ve("dve", "tensor_copy", [ba], [bb], b_[:, :d, :], a[:, :d, :])
                a, b_, ba, bb = b_, a, bb, ba
                d *= 2
            ve("dve", "memset", [], [b_offs], offs[:, 0:1, :], 0.0)
            ve("dve", "tensor_copy", [ba], [b_offs], offs[:, 1:, :], a[:, :NT - 1, :])
            ve("dve", "tensor_tensor", [b_cw, b_offs], [b_NC], NCt[:], cw[:], offs[:], ALU.add)
        P.barrier()

        xT = sb("xT", [128, KC, 512], F32); b_xT = P.bufs(KC, "xT")
        xnT2 = sb("xnT2", [128, KC, 512], BF16); b_xn2 = P.bufs(KC, "xn2")
        big = sb("big", [128, 32, 512], BF16); b_big = P.bufs(32, "big")
        maskt = sb("maskt", [128, 16, 513], BF16); b_mask = P.buf("mask")
        stage = [sb(f"stage{i}", [128, D], F32) for i in range(2)]; b_stage = P.bufs(2, "stage")
        NW = 4
        wbuf = [sb(f"wbuf{i}", [128, KC, GW], BF16) for i in range(NW)]; b_wbuf = P.bufs(NW, "wbuf")
        NKV = 4
        kbuf = [sb(f"kbuf{i}", [128, 512], BF16) for i in range(NKV)]; b_kbuf = P.bufs(NKV, "kbuf")
        vbuf = [sb(f"vbuf{i}", [128, 4, 128], BF16) for i in range(NKV)]; b_vbuf = P.bufs(NKV, "vbuf")
        NR = 3
        PT = [sb(f"PT{i}", [128, 512], BF16) for i in range(NR)]; b_PT = P.bufs(NR, "PT")
        AT = [sb(f"AT{i}", [128, 1024], BF16) for i in range(2)]; b_AT = P.bufs(2, "AT")
        SPb = [sb(f"SPb{i}", [128, 1024], BF16) for i in range(2)]; b_SPb = P.bufs(2, "SPb")
        E1 = sb("E1", [128, 1024], F32); b_E1 = P.buf("E1")
        SPa = [sb(f"SPa{i}", [128, 512], BF16) for i in range(2)]; b_SPa = P.bufs(2, "SPa")
        biasT = [sb(f"biasT{i}", [128, NT], F32) for i in range(2)]; b_biasT = P.bufs(2, "biasT")
        crowb = [sb(f"crowb{i}", [16, 512], BF16) for i in range(2)]; b_crowb = P.bufs(2, "crowb")
        sq = [sb(f"sq{i}", [128, 512], BF16) for i in range(4)]; b_sq = P.bufs(4, "sq")
        rrep = sb("rrep", [128, 512], F32); b_rrep = P.buf("rrep")
        Lacc = sb("Lacc", [128, 512], F32); b_Lacc = P.buf("Lacc")
        tmpf = [sb(f"tmpf{i}", [128, 512], F32) for i in range(4)]; b_tmpf = P.bufs(4, "tmpf")

        P.dma("sp", maskt[:], masks_d, key="mask", writes=[b_mask])

        wctr = [0]

        def wload(src_ap, kcx):
            i = wctr[0] % NW
            wctr[0] += 1
            P.dma("sp", wbuf[i][:, 0:kcx, :], src_ap, key=f"w{i}", reads=[b_wscr], writes=[b_wbuf[i]])
            return wbuf[i], b_wbuf[i]

        def rmsnorm_fm(g_col, dst, b_dst, in_place=False):
            bk = nbank()
            for kc in range(KC):
                i = kc % 4
                act(sq[i][:], xT[:, kc, :], AF.Square, [b_xT[kc]], [b_sq[i]])
                mm(PS[bk][:, :], ones_bf, sq[i][:], kc == 0, kc == KC - 1, [b_const, b_sq[i]], [PB[bk]])
            ve("dve", "tensor_scalar", [PB[bk]], [b_rrep], rrep[:], PS[bk][:, :], 1.0 / D, RMS_EPS, ALU.mult, ALU.add)
            act(rrep[:], rrep[:], AF.Sqrt, [b_rrep], [b_rrep])
            ve("dve", "reciprocal", [b_rrep], [b_rrep], rrep[:], rrep[:])
            for kc in range(KC):
                ve("dve", "scalar_tensor_tensor", [b_xT[kc], b_rrep, b_const], [b_dst[kc]],
                   dst[:, kc, :], xT[:, kc, :], g_col[:, kc:kc + 1], rrep[:], ALU.mult, ALU.mult)

        kvctr = [0]
        tctr = [0]

        for m in range(NSLOT):
            for sub in range(4):
                i = tctr[0] % 2; tctr[0] += 1
                r0 = m * 512 + sub * 128
                P.dma("sp", stage[i][:], x_own[r0:r0 + 128, :], key=f"stage{i}", writes=[b_stage[i]])
                for k0 in range(0, KC, 4):
                    bk = nbank()
                    kn = min(4, KC - k0)
                    for q in range(kn):
                        tr(PS[bk][:, q * 128:(q + 1) * 128], stage[i][:, (k0 + q) * 128:(k0 + q + 1) * 128],
                           ident_f, [b_stage[i], b_const], [PB[bk]])
                    ve("dve", "tensor_copy", [PB[bk]], b_xT[k0:k0 + kn],
                       xT[:, k0:k0 + kn, sub * 128:(sub + 1) * 128],
                       PS[bk][:, 0:kn * 128].rearrange("p (k t) -> p k t", t=128))
            rmsnorm_fm(gmix, xnT2, b_xn2)
            for g in range(NGQ):
                wb, bwb = wload(wq_s[g], KC)
                for hh in range(GW // 128):
                    hd = g * (GW // 128) + hh
                    bk = nbank()
                    for kc in range(KC):
                        mm(PS[bk][:, :], wb[:, kc, hh * 128:(hh + 1) * 128], xnT2[:, kc, :], kc == 0, kc == KC - 1,
                           [bwb, b_xn2[kc]], [PB[bk]])
                    act(big[:, hd, :], PS[bk][:, :], AF.Copy, [PB[bk]], [b_big[hd]], scale=SCALE)
            nkt = 16 * m + 16
            FS = (0, 1); FO = 2; SO = 3; FL = 0; MISC = 0
            def make_head(h):
                hp = h % 2
                ve("dve", "tensor_scalar", [b_NC, b_offs], [b_biasT[hp]], biasT[hp][:, 0:nkt], NCt[:, 0:nkt, h],
                   offs[:, 16 * m, h:h + 1], None, ALU.subtract)
                tr(PS[MISC][0:16, 0:128], NCt[:, 16 * m:16 * m + 16, h], ident_f, [b_NC, b_const], [PB[MISC]])
                for i4 in range(4):
                    ve("dve", "tensor_scalar", [PB[MISC], b_offs, b_const], [b_crowb[hp]],
                       crowb[hp][:, i4 * 128:(i4 + 1) * 128], PS[MISC][0:16, 0:128],
                       offs[0:16, 16 * m, h:h + 1], negselc[:, i4:i4 + 1], ALU.subtract, ALU.mult)
                qf = big[:, h, :]; qs = big[:, H + h, :]
                kv = {}

                def load_kv(mx, kb):
                    if (mx, kb) in kv:
                        return kv[(mx, kb)]
                    i = kvctr[0] % NKV; kvctr[0] += 1
                    P.dma("sp", kbuf[i][:], KT_s[mx, h, :, kb * 512:(kb + 1) * 512], key=f"kb{i}",
                          reads=[b_KV], writes=[b_kbuf[i]])
                    P.dma("sp", vbuf[i][:], V_s[mx, kb * 512:(kb + 1) * 512, h * 128:(h + 1) * 128]
                          .rearrange("(k p) d -> p k d", p=128), key=f"vb{i}", reads=[b_KV], writes=[b_vbuf[i]])
                    kv.clear()
                    kv[(mx, kb)] = i
                    return i

                npair = nkt // 2
                fst = {}
                sst = {}
                cur = {}

                def fox_A(t):
                    if t % 4 == 0:
                        cur["f"] = load_kv(0, t // 4)
                    ikv = cur["f"]
                    bS = FS[t % 2]
                    dg = t >= 16 * m
                    mm(PS[bS][:, :], kbuf[ikv][:, (t % 4) * 128:(t % 4 + 1) * 128], qf, True, False,
                       [b_kbuf[ikv], b_big[h]], [PB[bS]])
                    mm(PS[bS][:, :], ones_bf[0:16, :], crowb[hp][:, :], False, not dg,
                       [b_const, b_crowb[hp]], [PB[bS]])
                    if dg:
                        mm(PS[bS][:, :], ident_bf, maskt[:, t - 16 * m, 1:513], False, True,
                           [b_const, b_mask], [PB[bS]])
                    ip = t % NR
                    act(PT[ip][:], PS[bS][:, :], AF.Exp, [PB[bS], b_biasT[hp]], [b_PT[ip]],
                        bias=biasT[hp][:, t:t + 1])
                    fst[t] = (ikv, t % 4, ip)

                def fox_B(t):
                    ikv, kk, ip = fst.pop(t)
                    mm(PS[FO][:, :], vbuf[ikv][:, kk, :], PT[ip][:], t == 0, t == nkt - 1,
                       [b_vbuf[ikv], b_PT[ip]], [PB[FO]])
                    if t == 0:
                        ve("dve", "tensor_copy", [b_PT[ip]], [b_Lacc], Lacc[:], PT[ip][:])
                    else:
                        ve("dve", "tensor_tensor", [b_PT[ip], b_Lacc], [b_Lacc], Lacc[:], Lacc[:], PT[ip][:],
                           ALU.add)

                def sb_A(ps):
                    JA = nkt - 1 - 2 * ps
                    if JA % 4 == 3:
                        cur["s"] = load_kv(1, JA // 4)
                    ikv = cur["s"]
                    z = ps % 2
                    for half, J in ((0, JA), (1, JA - 1)):
                        bZ = 4 + 2 * z + half
                        dg = J >= 16 * m
                        mm(PS[bZ][:, :], kbuf[ikv][:, (J % 4) * 128:(J % 4 + 1) * 128], qs, True, False,
                           [b_kbuf[ikv], b_big[H + h]], [PB[bZ]], skip_group_check=True)
                        if dg:
                            mm(PS[bZ][:, :], ident_bf, maskt[:, J - 16 * m, 0:512], False, False,
                               [b_const, b_mask], [PB[bZ]], skip_group_check=True)
                    sst[ps] = (ikv, JA % 4, z)

                def sb_exp(ps):
                    ikv, kkA, z = sst[ps]
                    pbz = [PB[4 + 2 * z], PB[5 + 2 * z]]
                    act(E1[:], PSZ[z][:, :], AF.Exp, pbz, [b_E1])

                def sb_ln(ps):
                    ikv, kkA, z = sst[ps]
                    act(SPb[z][:], E1[:], AF.Ln, [b_E1], [b_SPb[z]], bias=1.0)

                def sb_B(ps):
                    ikv, kkA, z = sst[ps]
                    bA = 4 + 2 * z; bB = 5 + 2 * z
                    prev = (ps - 1) % 2
                    mm(PS[bA][:, :], negtri, SPb[z][:, 0:512], False, ps == 0, [b_const, b_SPb[z]], [PB[bA]],
                       skip_group_check=True)
                    if ps >= 1:
                        mm(PS[bA][:, :], negones, SPa[prev][:], False, True, [b_const, b_SPa[prev]], [PB[bA]],
                           skip_group_check=True)
                    mm(PS[bB][:, :], negtri, SPb[z][:, 512:1024], False, False, [b_const, b_SPb[z]], [PB[bB]],
                       skip_group_check=True)
                    mm(PS[bB][:, :], negones, SPb[z][:, 0:512], False, ps == 0, [b_const, b_SPb[z]], [PB[bB]],
                       skip_group_check=True)
                    if ps >= 1:
                        mm(PS[bB][:, :], negones, SPa[prev][:], False, True, [b_const, b_SPa[prev]], [PB[bB]],
                           skip_group_check=True)
                    act(AT[z][:], PSZ[z][:, :], AF.Exp, [PB[bA], PB[bB]], [b_AT[z]])
                    if ps < npair - 1:
                        if ps == 0:
                            ve("pool", "tensor_tensor", [b_SPb[z]], [b_SPa[0]], SPa[0][:], SPb[z][:, 0:512],
                               SPb[z][:, 512:1024], ALU.add)
                        else:
                            ve("pool", "tensor_tensor", [b_SPb[z], b_SPa[prev]], [b_SPa[ps % 2]],
                               SPa[ps % 2][:], SPa[prev][:], SPb[z][:, 0:512], ALU.add)
                            ve("pool", "tensor_tensor", [b_SPb[z], b_SPa[ps % 2]], [b_SPa[ps % 2]],
                               SPa[ps % 2][:], SPa[ps % 2][:], SPb[z][:, 512:1024], ALU.add)

                def sb_C(ps):
                    ikv, kkA, z = sst.pop(ps)
                    mm(PS[SO][:, :], vbuf[ikv][:, kkA, :], AT[z][:, 0:512], ps == 0, False,
                       [b_vbuf[ikv], b_AT[z]], [PB[SO]])
                    mm(PS[SO][:, :], vbuf[ikv][:, kkA - 1, :], AT[z][:, 512:1024], False, ps == npair - 1,
                       [b_vbuf[ikv], b_AT[z]], [PB[SO]])

                def step(ps):
                    if ps < npair:
                        fox_A(2 * ps)
                        sb_A(ps)
                        sb_exp(ps)
                        fox_A(2 * ps + 1)
                    if 1 <= ps <= npair:
                        sb_B(ps - 1)
                    if ps < npair:
                        sb_ln(ps)
                    if 1 <= ps <= npair:
                        fox_B(2 * ps - 1)
                    if ps < npair:
                        fox_B(2 * ps)
                    if ps >= 2:
                        sb_C(ps - 2)
                def fin_f():
                    i = h % 4
                    mm(PS[FL][:, :], ones_f, Lacc[:], True, True, [b_const, b_Lacc], [PB[FL]])
                    ve("dve", "reciprocal", [PB[FL]], [b_tmpf[i]], tmpf[i][:], PS[FL][:, :])
                    ve("dve", "tensor_tensor", [PB[FO], b_tmpf[i]], [b_big[2 * H + h]], big[:, 2 * H + h, :],
                       PS[FO][:, :], tmpf[i][:], ALU.mult)

                def fin_s():
                    ve("dve", "tensor_copy", [PB[SO]], [b_big[3 * H + h]], big[:, 3 * H + h, :], PS[SO][:, :])

                return step, fin_f, fin_s

            npair_m = nkt // 2
            prev = None
            for h in range(H):
                hd = make_head(h)
                for ps in range(npair_m):
                    if prev is not None and ps < 2:
                        prev[0](npair_m + ps)
                        if ps == 0:
                            prev[1]()
                        else:
                            prev[2]()
                    hd[0](ps)
                prev = hd
            prev[0](npair_m); prev[1](); prev[0](npair_m + 1); prev[2]()
            for g in range(NGD):
                wA, bA = wload(wof_s[g], H)
                wB, bB = wload(wos_s[g], H)
                wGa, bGa = wload(wga_s[g], KC)
                wGb, bGb = wload(wgb_s[g], KC)
                for cc in range(GW // 128):
                    c = g * (GW // 128) + cc
                    cs = slice(cc * 128, (cc + 1) * 128)
                    byA = nbank(); byB = nbank(); bgA = nbank(); bgB = nbank()
                    for h in range(H):
                        mm(PS[byA][:, :], wA[:, h, cs], big[:, 2 * H + h, :], h == 0, h == H - 1,
                           [bA, b_big[2 * H + h]], [PB[byA]])
                    for h in range(H):
                        mm(PS[byB][:, :], wB[:, h, cs], big[:, 3 * H + h, :], h == 0, h == H - 1,
                           [bB, b_big[3 * H + h]], [PB[byB]])
                    for kc in range(KC):
                        mm(PS[bgA][:, :], wGa[:, kc, cs], xnT2[:, kc, :], kc == 0, kc == KC - 1,
                           [bGa, b_xn2[kc]], [PB[bgA]])
                    for kc in range(KC):
                        mm(PS[bgB][:, :], wGb[:, kc, cs], xnT2[:, kc, :], kc == 0, kc == KC - 1,
                           [bGb, b_xn2[kc]], [PB[bgB]])
                    act(tmpf[0][:], PS[bgA][:, :], AF.Sigmoid, [PB[bgA]], [b_tmpf[0]])
                    act(tmpf[1][:], PS[bgB][:, :], AF.Sigmoid, [PB[bgB]], [b_tmpf[1]])
                    ve("dve", "tensor_tensor", [PB[byA], b_tmpf[0]], [b_tmpf[2]], tmpf[2][:], PS[byA][:, :],
                       tmpf[0][:], ALU.mult)
                    ve("dve", "tensor_tensor", [PB[byB], b_tmpf[1]], [b_tmpf[3]], tmpf[3][:], PS[byB][:, :],
                       tmpf[1][:], ALU.mult)
                    ve("dve", "tensor_tensor", [b_tmpf[2], b_tmpf[3]], [b_big[c]], big[:, c, :], tmpf[2][:],
                       tmpf[3][:], ALU.add)
            for g in range(NGD):
                wb, bwb = wload(wo_s[g], KC)
                for cc in range(GW // 128):
                    c = g * (GW // 128) + cc
                    bk = nbank()
                    for kc in range(KC):
                        mm(PS[bk][:, :], wb[:, kc, cc * 128:(cc + 1) * 128], big[:, kc, :], kc == 0, kc == KC - 1,
                           [bwb, b_big[kc]], [PB[bk]])
                    ve("dve", "tensor_tensor", [PB[bk], b_xT[c]], [b_xT[c]], xT[:, c, :], xT[:, c, :], PS[bk][:, :],
                       ALU.add)
            rmsnorm_fm(gmlp, xnT2, b_xn2)
            for half in range(2):
                for g in range(HJ * 128 // GW):
                    wb, bwb = wload(wup_s[half * (HJ * 128 // GW) + g], KC)
                    for jj in range(GW // 128):
                        j = g * (GW // 128) + jj
                        bk = nbank()
                        for kc in range(KC):
                            mm(PS[bk][:, :], wb[:, kc, jj * 128:(jj + 1) * 128], xnT2[:, kc, :], kc == 0,
                               kc == KC - 1, [bwb, b_xn2[kc]], [PB[bk]])
                        i = j % 4
                        act(tmpf[i][:], PS[bk][:, :], AF.Square, [PB[bk]], [b_tmpf[i]])
                        ve("dve", "scalar_tensor_tensor", [PB[bk], b_tmpf[i]], [b_big[j]], big[:, j, :],
                           PS[bk][:, :], 0.0, tmpf[i][:], ALU.is_gt, ALU.mult)
                for g in range(NGD):
                    pieces = []
                    step_k = min(KC, HJ)
                    for k0 in range(0, HJ, step_k):
                        wb, bwb = wload(wdn_s[g, :, half * HJ + k0: half * HJ + k0 + step_k, :], step_k)
                        pieces.append((wb, bwb, k0, step_k))
                    for cc in range(GW // 128):
                        c = g * (GW // 128) + cc
                        bk = nbank()
                        n = 0
                        for (wb, bwb, k0, sk) in pieces:
                            for kk in range(sk):
                                mm(PS[bk][:, :], wb[:, kk, cc * 128:(cc + 1) * 128], big[:, k0 + kk, :], n == 0,
                                   n == HJ - 1, [bwb, b_big[k0 + kk]], [PB[bk]])
                                n += 1
                        ve("dve", "tensor_tensor", [PB[bk], b_xT[c]], [b_xT[c]], xT[:, c, :], xT[:, c, :],
                           PS[bk][:, :], ALU.add)
            rmsnorm_fm(gfin, xT, b_xT)
            for sub in range(4):
                i = tctr[0] % 2; tctr[0] += 1
                for k0 in range(0, KC, 4):
                    bk = nbank()
                    kn = min(4, KC - k0)
                    for q in range(kn):
                        tr(PS[bk][:, q * 128:(q + 1) * 128], xT[:, k0 + q, sub * 128:(sub + 1) * 128], ident_f,
                           [b_xT[k0 + q], b_const], [PB[bk]])
                    act(stage[i][:, k0 * 128:(k0 + kn) * 128], PS[bk][:, 0:kn * 128], AF.Copy, [PB[bk]],
                        [b_stage[i]])
                r0 = m * 512 + sub * 128
                P.dma("pool", out_d[r0:r0 + 128, :], stage[i][:], key=f"out{i}", reads=[b_stage[i]],
                      writes=[P.buf()])
        P.barrier()
        P.emit()
    return nc, P


def host_consts(r):
    cbf = np.zeros((128, 4, 128), np.float32)
    cbf[:, 0, :] = np.eye(128)
    j = np.arange(128)[:, None]; s = np.arange(128)[None, :]
    cbf[:, 1, :] = -(j >= s).astype(np.float32)
    cbf[:, 2, :] = -1.0
    cbf[:, 3, :] = 1.0
    cf32 = np.zeros((128, 3, 128), np.float32)
    cf32[:, 0, :] = np.eye(128)
    cf32[:, 1, :] = (j <= s).astype(np.float32)
    cf32[:, 2, :] = 1.0
    sel = np.zeros((16, 4), np.float32)
    for i in range(4):
        sel[4 * r + i, i] = -1.0
    sidx = np.arange(128)[:, None, None]
    idx = np.arange(16)[None, :, None]
    col = np.arange(513)[None, None, :]
    keypos = 128 * idx + sidx
    qpos = 512 * r + col - 1
    masks = np.where(keypos > qpos, NEG, 0.0).astype(np.float32)
    bf = ml_dtypes.bfloat16
    return cbf.astype(bf), cf32, sel, masks.astype(bf)


_CACHE = {}


def run(cfg, x, norm_mix_g, w_in, b_forget, w_out_fox, w_out_sb, w_out, norm_mlp_g, w_mlp_up, w_mlp_down,
        norm_final_g):
    D = cfg["D"]; H = cfg["H"]; NSLOT = cfg["NSLOT"]; KC = D // 128
    S = 2048 * NSLOT
    key = tuple(sorted(cfg.items()))
    if key not in _CACHE:
        _CACHE[key] = build(cfg)
    nc, P = _CACHE[key]
    f = lambda a: np.ascontiguousarray(np.asarray(a, dtype=np.float32))
    x = f(x)

    def col(g):
        return np.ascontiguousarray(f(g).reshape(KC, 128).T)

    in_maps = []
    for c in range(8):
        b = c // 4; r = c % 4
        cbf, cf32, sel, masks = host_consts(r)
        xb = x[b]
        own = np.ascontiguousarray(xb.reshape(NSLOT, 4, 512, D)[:, r].reshape(NSLOT * 512, D))
        in_maps.append(dict(
            x_seq=xb, x_own=own, w_in=f(w_in[0]), w_of=f(w_out_fox[0]), w_os=f(w_out_sb[0]), w_o=f(w_out[0]),
            w_up=f(w_mlp_up[0]), w_dn=f(w_mlp_down[0]), gmix=col(norm_mix_g[0]), gmlp=col(norm_mlp_g[0]),
            gfin=col(norm_final_g), bfg=np.ascontiguousarray(np.broadcast_to(f(b_forget[0])[None, :], (128, H))),
            masks=masks, sel=sel, cbf=cbf, cf32=cf32))
    res = run_bass_kernel_spmd(nc, in_maps, core_ids=list(range(8)))
    out = np.empty((2, S, D), np.float32)
    for c in range(8):
        b = c // 4; r = c % 4
        o = np.asarray(res.results[c]["out"]).reshape(NSLOT, 512, D)
        out[b].reshape(NSLOT, 4, 512, D)[:, r] = o
    return out


def kernel(**inputs):
    return run(FULL, **inputs)
```

```python
import contextlib
import numpy as np
import ml_dtypes
import concourse.bass as bass
import concourse.mybir as mybir
from concourse.bass_utils import run_bass_kernel_spmd

F32 = mybir.dt.float32
BF16 = mybir.dt.bfloat16
AF = mybir.ActivationFunctionType
ALU = mybir.AluOpType

ENGS = ("pe", "act", "dve", "pool", "sp")
EPOCH = 20000
DMA_EPOCH = 1500
NEG = -30000.0
RMS_EPS = 1e-6


class Buf:
    __slots__ = ("name", "writers", "readers")

    def __init__(self, name):
        self.name = name
        self.writers = []
        self.readers = []


class Op:
    __slots__ = ("eng", "fn", "is_dma", "key", "deps", "flag", "cnt", "sem", "barrier")

    def __init__(self, eng, fn, is_dma=False, key=None):
        self.eng = eng
        self.fn = fn
        self.is_dma = is_dma
        self.key = key
        self.deps = []
        self.flag = False
        self.cnt = None
        self.sem = None
        self.barrier = None


class Prog:
    def __init__(self, nc):
        self.nc = nc
        self.ops = {e: [] for e in ENGS}
        self.dma_keys = {}
        self.nbuf = 0
        self.n_ops = 0

    def buf(self, name=None):
        self.nbuf += 1
        return Buf(name or f"b{self.nbuf}")

    def bufs(self, n, name="b"):
        return [self.buf(f"{name}{i}") for i in range(n)]

    def _track(self, op, reads, writes):
        deps = []
        for b in reads:
            deps.extend(b.writers)
            b.readers.append(op)
        for b in writes:
            if b.readers:
                deps.extend(b.readers)
                deps.extend(b.writers)
                b.writers = [op]
                b.readers = []
            elif b.writers and not all(w.eng == op.eng for w in b.writers):
                deps.extend(b.writers)
                b.writers = [op]
            elif op.is_dma or any(w.is_dma for w in b.writers):
                b.writers.append(op)
            else:
                b.writers = [op]
        out = []
        seen = set()
        for d in deps:
            if d is op or id(d) in seen:
                continue
            seen.add(id(d))
            out.append(d)
        return out

    def op(self, eng, fn, reads=(), writes=()):
        o = Op(eng, fn)
        raw = set()
        for b in reads:
            for w in b.writers:
                raw.add(id(w))
        deps = self._track(o, reads, writes)
        keep = []
        for d in deps:
            if (not d.is_dma) and d.eng == eng:
                if eng == "pe" or id(d) not in raw:
                    continue
            keep.append(d)
            d.flag = True
        o.deps = keep
        self.ops[eng].append(o)
        self.n_ops += 1
        return o

    def dma(self, eng, out, in_, key, reads=(), writes=(), **kw):
        def fn(e):
            return e.dma_start(out=out, in_=in_, **kw)
        o = Op(eng, fn, is_dma=True, key=key)
        o.deps = self._track(o, reads, writes)
        for d in o.deps:
            d.flag = True
        self.ops[eng].append(o)
        self.dma_keys.setdefault(key, []).append(o)
        self.n_ops += 1
        return o

    def barrier(self):
        targets = []
        for e in ENGS:
            for o in reversed(self.ops[e]):
                if (not o.is_dma) and o.barrier is None:
                    o.flag = True
                    targets.append(o)
                    break
        for lst in self.dma_keys.values():
            if lst:
                targets.append(lst[-1])
        for e in ENGS:
            o = Op(e, None)
            o.barrier = list(targets)
            self.ops[e].append(o)

    def emit(self):
        nc = self.nc
        sems = {}

        def get_sem(name):
            if name not in sems:
                sems[name] = nc.alloc_semaphore(name=name)
            return sems[name]

        for e in ENGS:
            c = 0
            ep = 0
            for o in self.ops[e]:
                if o.is_dma or o.barrier is not None:
                    continue
                if o.flag:
                    c += 1
                    if c > EPOCH:
                        ep += 1
                        c = 1
                    o.sem = f"c_{e}_{ep}"
                    o.cnt = c
        for key, lst in self.dma_keys.items():
            c = 0
            ep = 0
            for o in lst:
                c += 1
                if c > DMA_EPOCH:
                    ep += 1
                    c = 1
                o.sem = f"d_{key}_{ep}"
                o.cnt = 16 * c

        def emit_engine(ename, e):
            waited = {}
            for o in self.ops[ename]:
                deps = o.barrier if o.barrier is not None else o.deps
                need = {}
                for d in deps:
                    if d.sem is None:
                        continue
                    if need.get(d.sem, 0) < d.cnt:
                        need[d.sem] = d.cnt
                for s, v in need.items():
                    if waited.get(s, 0) >= v:
                        continue
                    e.wait_ge(get_sem(s), v)
                    waited[s] = v
                if o.barrier is not None:
                    continue
                inst = o.fn(e)
                if o.is_dma:
                    inst.then_inc(get_sem(o.sem), 16)
                elif o.flag:
                    inst.then_inc(get_sem(o.sem), 1)

        with nc.Block() as block:
            @block.tensor
            def _(e):
                emit_engine("pe", e)

            @block.scalar
            def _(e):
                emit_engine("act", e)

            @block.vector
            def _(e):
                emit_engine("dve", e)

            @block.gpsimd
            def _(e):
                emit_engine("pool", e)

            @block.sync
            def _(e):
                emit_engine("sp", e)
        self.n_sems = len(sems)


FULL = dict(D=2048, H=8, DFF=8192, NSLOT=8)


def build(cfg):
    D = cfg["D"]; H = cfg["H"]; DFF = cfg["DFF"]; NSLOT = cfg["NSLOT"]
    KC = D // 128; W = H * 128; JC = DFF // 128
    S = 2048 * NSLOT; NB = S // 512; NT = S // 128; NOWN = 512 * NSLOT
    GW = 256
    HJ = JC // 2
    assert HJ <= 32 and HJ % KC == 0 or HJ <= KC
    IN_COLS = 6 * W + H + 2 * D
    oqf = 0; okf = W; ovf = 2 * W; off_ = 3 * W
    oqs = 3 * W + H; oks = oqs + W; ovs = oks + W; oga = ovs + W; ogb = oga + D
    SCALE = 128 ** -0.5

    nc = bass.Bass("TRN2", target_bir_lowering=False)
    P = Prog(nc)

    def din(name, shape, dt=F32):
        return nc.dram_tensor(name, list(shape), dt, kind="ExternalInput").ap()

    def dscr(name, shape, dt=BF16):
        return nc.dram_tensor(name, list(shape), dt, kind="Internal").ap()

    x_seq = din("x_seq", [S, D]); x_own = din("x_own", [NOWN, D])
    w_in = din("w_in", [D, IN_COLS]); w_of = din("w_of", [W, D]); w_os = din("w_os", [W, D])
    w_o = din("w_o", [D, D]); w_up = din("w_up", [D, DFF]); w_dn = din("w_dn", [DFF, D])
    gmix_d = din("gmix", [128, KC]); gmlp_d = din("gmlp", [128, KC]); gfin_d = din("gfin", [128, KC])
    bfg_d = din("bfg", [128, H])
    masks_d = din("masks", [128, 16, 513], BF16)
    sel_d = din("sel", [16, 4], F32)
    cbf_d = din("cbf", [128, 4, 128], BF16)
    cf32_d = din("cf32", [128, 3, 128], F32)
    out_d = nc.dram_tensor("out", [NOWN, D], F32, kind="ExternalOutput").ap()

    NGQ = 2 * W // GW; NGD = D // GW; NGU = DFF // GW
    wq_s = dscr("wq_s", [NGQ, 128, KC, GW]); wga_s = dscr("wga_s", [NGD, 128, KC, GW])
    wgb_s = dscr("wgb_s", [NGD, 128, KC, GW]); wof_s = dscr("wof_s", [NGD, 128, H, GW])
    wos_s = dscr("wos_s", [NGD, 128, H, GW]); wo_s = dscr("wo_s", [NGD, 128, KC, GW])
    wup_s = dscr("wup_s", [NGU, 128, KC, GW]); wdn_s = dscr("wdn_s", [NGD, 128, JC, GW])
    KT_s = dscr("KT_s", [2, H, 128, S]); V_s = dscr("V_s", [2, S, W])

    es = contextlib.ExitStack()

    def sb(name, shape, dt, stack=None):
        return (stack or es).enter_context(nc.sbuf_tensor("s_" + name, list(shape), dt))

    with es:
        PSZ = [es.enter_context(nc.psum_tensor(f"psz{i}", [128, 1024], F32)) for i in range(2)]
        PSa = [es.enter_context(nc.psum_tensor(f"ps{i}", [128, 512], F32)) for i in range(4)]
        PS = [p[:, :] for p in PSa] + [PSZ[0][:, 0:512], PSZ[0][:, 512:1024], PSZ[1][:, 0:512], PSZ[1][:, 512:1024]]
        PB = P.bufs(8, "psb")
        rr = [0]

        def nbank():
            i = rr[0] % 8
            rr[0] += 1
            return i

        cbf = sb("cbf", [128, 4, 128], BF16); cf32 = sb("cf32", [128, 3, 128], F32)
        negselc = sb("negselc", [16, 4], F32)
        gmix = sb("gmix", [128, KC], F32); gmlp = sb("gmlp", [128, KC], F32); gfin = sb("gfin", [128, KC], F32)
        bfg = sb("bfg", [128, H], F32)
        NCt = sb("NCt", [128, NT, H], F32); offs = sb("offs", [128, NT, H], F32)
        fl = sb("fl", [128, NT, H], F32)
        b_const = P.buf("const"); b_NC = P.buf("NC"); b_offs = P.buf("offs"); b_fl = P.buf("fl")
        ident_bf = cbf[:, 0, :]; negtri = cbf[:, 1, :]; negones = cbf[:, 2, :]; ones_bf = cbf[:, 3, :]
        ident_f = cf32[:, 0, :]; tri_f = cf32[:, 1, :]; ones_f = cf32[:, 2, :]

        for dst, src in ((cbf, cbf_d), (cf32, cf32_d), (negselc, sel_d), (gmix, gmix_d), (gmlp, gmlp_d),
                         (gfin, gfin_d), (bfg, bfg_d)):
            P.dma("sp", dst[:], src, key="const", writes=[b_const])

        def mm(out, lhsT, rhs, start, stop, reads, writes, **kw):
            P.op("pe", lambda e: e.matmul(out, lhsT, rhs, start=start, stop=stop, **kw), reads, writes)

        def tr(out, in_, ident, reads, writes):
            P.op("pe", lambda e: e.transpose(out, in_, ident), reads, writes)

        def act(out, in_, func, reads, writes, **kw):
            P.op("act", lambda e: e.activation(out, in_, func, **kw), reads, writes)

        def ve(eng, name, reads, writes, *a, **kw):
            P.op(eng, lambda e: getattr(e, name)(*a, **kw), reads, writes)

        b_wscr = P.buf("wscr")

        def cast_groups(w_ap, r0, nrows, c0, ncols, dst, gbase):
            kcx = nrows // 128
            step = 16 if kcx > 16 else kcx
            for g in range(ncols // GW):
                for k0 in range(0, kcx, step):
                    src = w_ap[r0 + k0 * 128: r0 + (k0 + step) * 128, c0 + g * GW: c0 + (g + 1) * GW]
                    src = src.rearrange("(kc f) c -> f kc c", f=128)
                    P.dma("pool", dst[gbase + g, :, k0:k0 + step, :], src, key="cast", writes=[b_wscr])

        with contextlib.ExitStack() as s1:
            wkv = sb("wkv", [128, KC, 2 * W + 8], BF16, s1)
            xst = [sb(f"xst{i}", [128, D], F32, s1) for i in range(4)]
            xs = [sb(f"xs{i}", [128, D], BF16, s1) for i in range(4)]
            junk = sb("junk", [128, D], BF16, s1)
            ssq = sb("ssq", [128, 8], F32, s1); rstd = sb("rstd", [128, 8], F32, s1)
            xnT = [sb(f"xnT{i}", [128, KC, 512], BF16, s1) for i in range(2)]
            kst = [sb(f"kst{i}", [128, H, 512], BF16, s1) for i in range(2)]
            vst = [sb(f"vst{i}", [128, 4, W], BF16, s1) for i in range(2)]
            b_wkv = P.buf("wkv"); b_xst = P.bufs(4, "xst"); b_xs = P.bufs(4, "xs"); b_junk = P.buf("junk")
            b_ssq = P.bufs(8, "ssq"); b_rstd = P.bufs(8, "rstd"); b_xnT = P.bufs(2, "xnT")
            b_kst = P.bufs(2, "kst"); b_vst = P.bufs(2, "vst"); b_KV = P.buf("KVs")

            cnt = [0]

            stg = [0]

            def load_wkv(mx):
                ok_, ov_ = (okf, ovf) if mx == 0 else (oks, ovs)
                ks_ = D // W
                for (c0, dcol) in ((ok_, 0), (ov_, W)):
                    for k0 in range(0, KC, ks_):
                        i = stg[0] % 4; stg[0] += 1
                        src = w_in[k0 * 128:(k0 + ks_) * 128, c0:c0 + W].rearrange("(kc f) c -> f kc c", f=128)
                        view = xst[i][:].rearrange("p (k c) -> p k c", k=ks_)
                        P.dma("sp", view, src, key=f"xst{i}", writes=[b_xst[i]])
                        ve("dve", "tensor_copy", [b_xst[i]], [b_wkv], wkv[:, k0:k0 + ks_, dcol:dcol + W], view)
                if mx == 0:
                    src = w_in[:, off_:off_ + H].rearrange("(kc f) c -> f kc c", f=128)
                    P.dma("pool", wkv[:, :, 2 * W:2 * W + H], src, key="wkv", writes=[b_wkv])

            def pre(gb):
                blk = gb % NB
                for sub in range(4):
                    i = cnt[0] % 4; j = cnt[0] % 8; cnt[0] += 1
                    r0 = blk * 512 + sub * 128
                    P.dma("sp", xst[i][:], x_seq[r0:r0 + 128, :], key=f"xst{i}", writes=[b_xst[i]])
                    act(junk[:], xst[i][:], AF.Square, [b_xst[i]], [b_junk, b_ssq[j]],
                        accum_out=ssq[:, j:j + 1])
                    ve("dve", "tensor_scalar", [b_ssq[j]], [b_rstd[j]], rstd[:, j:j + 1], ssq[:, j:j + 1],
                       1.0 / D, RMS_EPS, ALU.mult, ALU.add)
                    act(rstd[:, j:j + 1], rstd[:, j:j + 1], AF.Sqrt, [b_rstd[j]], [b_rstd[j]])
                    ve("dve", "reciprocal", [b_rstd[j]], [b_rstd[j]], rstd[:, j:j + 1], rstd[:, j:j + 1])
                    ve("dve", "tensor_scalar", [b_xst[i], b_rstd[j]], [b_xs[sub]], xs[sub][:], xst[i][:],
                       rstd[:, j:j + 1], None, ALU.mult)

            def trans(gb):
                xn = xnT[gb % 2]; bxn = b_xnT[gb % 2]
                for kc in range(KC):
                    bk = nbank()
                    for sub in range(4):
                        mm(PS[bk][:, sub * 128:(sub + 1) * 128], xs[sub][:, kc * 128:(kc + 1) * 128],
                           ident_bf, True, True, [b_xs[sub], b_const], [PB[bk]], skip_group_check=True)
                    act(xn[:, kc, :], PS[bk][:, 0:512], AF.Copy, [PB[bk], b_const], [bxn],
                        scale=gmix[:, kc:kc + 1])

            def kproj(gb):
                mx = gb // NB; blk = gb % NB
                xn = xnT[gb % 2]; bxn = b_xnT[gb % 2]
                ks = kst[gb % 2]; bks = b_kst[gb % 2]
                for h in range(H):
                    bk = nbank()
                    for kc in range(KC):
                        mm(PS[bk][:, :], wkv[:, kc, h * 128:(h + 1) * 128], xn[:, kc, :], kc == 0, kc == KC - 1,
                           [b_wkv, bxn], [PB[bk]])
                    ve("dve", "tensor_copy", [PB[bk]], [bks], ks[:, h, :], PS[bk][:, :])
                P.dma("pool", KT_s[mx, :, :, blk * 512:(blk + 1) * 512].rearrange("h d t -> d h t"), ks[:],
                      key=f"kst{gb % 2}", reads=[bks], writes=[b_KV])

            def vproj(gb):
                mx = gb // NB; blk = gb % NB
                xn = xnT[gb % 2]; bxn = b_xnT[gb % 2]
                vs = vst[gb % 2]; bvs = b_vst[gb % 2]
                for sub in range(4):
                    for cg in range(W // 512):
                        bk = nbank()
                        for kc in range(KC):
                            mm(PS[bk][:, :], xn[:, kc, sub * 128:(sub + 1) * 128],
                               wkv[:, kc, W + cg * 512: W + (cg + 1) * 512], kc == 0, kc == KC - 1,
                               [b_wkv, bxn], [PB[bk]])
                        if (sub + cg) % 2 == 0:
                            act(vs[:, sub, cg * 512:(cg + 1) * 512], PS[bk][:, :], AF.Copy, [PB[bk]], [bvs])
                        else:
                            ve("dve", "tensor_copy", [PB[bk]], [bvs], vs[:, sub, cg * 512:(cg + 1) * 512],
                               PS[bk][:, :])
                P.dma("pool", V_s[mx, blk * 512:(blk + 1) * 512, :].rearrange("(s p) c -> p s c", p=128),
                      vs[:], key=f"vst{gb % 2}", reads=[bvs], writes=[b_KV])
                if mx == 0:
                    for sub in range(4):
                        bk = nbank()
                        for kc in range(KC):
                            mm(PS[bk][:, 0:H], xn[:, kc, sub * 128:(sub + 1) * 128],
                               wkv[:, kc, 2 * W:2 * W + H], kc == 0, kc == KC - 1, [b_wkv, bxn], [PB[bk]])
                        ve("dve", "tensor_tensor", [PB[bk], b_const], [b_fl], fl[:, blk * 4 + sub, :],
                           PS[bk][:, 0:H], bfg[:, :], ALU.add)

            load_wkv(0)
            pre(0)
            for i in range(2):
                cast_groups(w_in, 0, D, (oqf, oqs)[i], W, wq_s, i * (W // GW))
            cast_groups(w_in, 0, D, oga, D, wga_s, 0)
            cast_groups(w_in, 0, D, ogb, D, wgb_s, 0)
            cast_groups(w_of, 0, W, 0, D, wof_s, 0)
            cast_groups(w_os, 0, W, 0, D, wos_s, 0)
            cast_groups(w_o, 0, D, 0, D, wo_s, 0)
            cast_groups(w_up, 0, D, 0, DFF, wup_s, 0)
            cast_groups(w_dn, 0, DFF, 0, D, wdn_s, 0)
            trans(0)
            for gb in range(2 * NB):
                if gb == NB:
                    load_wkv(1)
                if gb + 1 < 2 * NB:
                    pre(gb + 1)
                kproj(gb)
                if gb + 1 < 2 * NB:
                    trans(gb + 1)
                vproj(gb)
        P.barrier()

        NF = NT * H
        with contextlib.ExitStack() as s2:
            e1 = sb("e1", [128, NF], F32, s2); l1 = sb("l1", [128, NF], F32, s2)
            cw = sb("cw", [128, NT, H], F32, s2)
            sa = sb("sa", [128, NT, H], F32, s2); sbb = sb("sbb", [128, NT, H], F32, s2)
            b_e1 = P.buf(); b_l1 = P.buf(); b_cw = P.buf(); b_sa = P.buf(); b_sb = P.buf()
            flf = fl[:].rearrange("p j h -> p (j h)")
            act(e1[:], flf, AF.Exp, [b_fl], [b_e1], scale=-1.0)
            act(l1[:], e1[:], AF.Ln, [b_e1], [b_l1], bias=1.0)
            cwf = cw[:].rearrange("p j h -> p (j h)"); saf = sa[:].rearrange("p j h -> p (j h)")
            for c0 in range(0, NF, 512):
                c1 = min(NF, c0 + 512)
                bk = nbank()
                mm(PS[bk][:, 0:c1 - c0], tri_f, l1[:, c0:c1], True, True, [b_const, b_l1], [PB[bk]])
                ve("dve", "tensor_copy", [PB[bk]], [b_cw], cwf[:, c0:c1], PS[bk][:, 0:c1 - c0])
                bk = nbank()
                mm(PS[bk][:, 0:c1 - c0], ones_f, l1[:, c0:c1], True, True, [b_const, b_l1], [PB[bk]])
                ve("dve", "tensor_copy", [PB[bk]], [b_sa], saf[:, c0:c1], PS[bk][:, 0:c1 - c0])
            a, b_, ba, bb = sa, sbb, b_sa, b_sb
            d = 1
            while d < NT:
                ve("dve", "tensor_tensor", [ba], [bb], b_[:, d:, :], a[:, d:, :], a[:, :NT - d, :], ALU.add)
                ve("dve", "tensor_copy", [ba], [bb], b_[:, :d, :], a[:, :d, :])
                a, b_, ba, bb = b_, a, bb, ba
                d *= 2
            ve("dve", "memset", [], [b_offs], offs[:, 0:1, :], 0.0)
            ve("dve", "tensor_copy", [ba], [b_offs], offs[:, 1:, :], a[:, :NT - 1, :])
            ve("dve", "tensor_tensor", [b_cw, b_offs], [b_NC], NCt[:], cw[:], offs[:], ALU.add)
        P.barrier()

        xT = sb("xT", [128, KC, 512], F32); b_xT = P.bufs(KC, "xT")
        xnT2 = sb("xnT2", [128, KC, 512], BF16); b_xn2 = P.bufs(KC, "xn2")
        big = sb("big", [128, 32, 512], BF16); b_big = P.bufs(32, "big")
        maskt = sb("maskt", [128, 16, 513], BF16); b_mask = P.buf("mask")
        stage = [sb(f"stage{i}", [128, D], F32) for i in range(2)]; b_stage = P.bufs(2, "stage")
        NW = 4
        wbuf = [sb(f"wbuf{i}", [128, KC, GW], BF16) for i in range(NW)]; b_wbuf = P.bufs(NW, "wbuf")
        NKV = 4
        kbuf = [sb(f"kbuf{i}", [128, 512], BF16) for i in range(NKV)]; b_kbuf = P.bufs(NKV, "kbuf")
        vbuf = [sb(f"vbuf{i}", [128, 4, 128], BF16) for i in range(NKV)]; b_vbuf = P.bufs(NKV, "vbuf")
        NR = 3
        PT = [sb(f"PT{i}", [128, 512], BF16) for i in range(NR)]; b_PT = P.bufs(NR, "PT")
        AT = [sb(f"AT{i}", [128, 1024], BF16) for i in range(2)]; b_AT = P.bufs(2, "AT")
        SPb = [sb(f"SPb{i}", [128, 1024], BF16) for i in range(2)]; b_SPb = P.bufs(2, "SPb")
        E1 = sb("E1", [128, 1024], F32); b_E1 = P.buf("E1")
        SPa = [sb(f"SPa{i}", [128, 512], BF16) for i in range(2)]; b_SPa = P.bufs(2, "SPa")
        biasT = [sb(f"biasT{i}", [128, NT], F32) for i in range(2)]; b_biasT = P.bufs(2, "biasT")
        crowb = [sb(f"crowb{i}", [16, 512], BF16) for i in range(2)]; b_crowb = P.bufs(2, "crowb")
        sq = [sb(f"sq{i}", [128, 512], BF16) for i in range(4)]; b_sq = P.bufs(4, "sq")
        rrep = sb("rrep", [128, 512], F32); b_rrep = P.buf("rrep")
        Lacc = sb("Lacc", [128, 512], F32); b_Lacc = P.buf("Lacc")
        tmpf = [sb(f"tmpf{i}", [128, 512], F32) for i in range(4)]; b_tmpf = P.bufs(4, "tmpf")

        P.dma("sp", maskt[:], masks_d, key="mask", writes=[b_mask])

        wctr = [0]

        def wload(src_ap, kcx):
            i = wctr[0] % NW
            wctr[0] += 1
            P.dma("sp", wbuf[i][:, 0:kcx, :], src_ap, key=f"w{i}", reads=[b_wscr], writes=[b_wbuf[i]])
            return wbuf[i], b_wbuf[i]

        def rmsnorm_fm(g_col, dst, b_dst, in_place=False):
            bk = nbank()
            for kc in range(KC):
                i = kc % 4
                act(sq[i][:], xT[:, kc, :], AF.Square, [b_xT[kc]], [b_sq[i]])
                mm(PS[bk][:, :], ones_bf, sq[i][:], kc == 0, kc == KC - 1, [b_const, b_sq[i]], [PB[bk]])
            ve("dve", "tensor_scalar", [PB[bk]], [b_rrep], rrep[:], PS[bk][:, :], 1.0 / D, RMS_EPS, ALU.mult, ALU.add)
            act(rrep[:], rrep[:], AF.Sqrt, [b_rrep], [b_rrep])
            ve("dve", "reciprocal", [b_rrep], [b_rrep], rrep[:], rrep[:])
            for kc in range(KC):
                ve("dve", "scalar_tensor_tensor", [b_xT[kc], b_rrep, b_const], [b_dst[kc]],
                   dst[:, kc, :], xT[:, kc, :], g_col[:, kc:kc + 1], rrep[:], ALU.mult, ALU.mult)

        kvctr = [0]
        tctr = [0]

        for m in range(NSLOT):
            for sub in range(4):
                i = tctr[0] % 2; tctr[0] += 1
                r0 = m * 512 + sub * 128
                P.dma("sp", stage[i][:], x_own[r0:r0 + 128, :], key=f"stage{i}", writes=[b_stage[i]])
                for k0 in range(0, KC, 4):
                    bk = nbank()
                    kn = min(4, KC - k0)
                    for q in range(kn):
                        tr(PS[bk][:, q * 128:(q + 1) * 128], stage[i][:, (k0 + q) * 128:(k0 + q + 1) * 128],
                           ident_f, [b_stage[i], b_const], [PB[bk]])
                    ve("dve", "tensor_copy", [PB[bk]], b_xT[k0:k0 + kn],
                       xT[:, k0:k0 + kn, sub * 128:(sub + 1) * 128],
                       PS[bk][:, 0:kn * 128].rearrange("p (k t) -> p k t", t=128))
            rmsnorm_fm(gmix, xnT2, b_xn2)
            for g in range(NGQ):
                wb, bwb = wload(wq_s[g], KC)
                for hh in range(GW // 128):
                    hd = g * (GW // 128) + hh
                    bk = nbank()
                    for kc in range(KC):
                        mm(PS[bk][:, :], wb[:, kc, hh * 128:(hh + 1) * 128], xnT2[:, kc, :], kc == 0, kc == KC - 1,
                           [bwb, b_xn2[kc]], [PB[bk]])
                    act(big[:, hd, :], PS[bk][:, :], AF.Copy, [PB[bk]], [b_big[hd]], scale=SCALE)
            nkt = 16 * m + 16
            FS = (0, 1); FO = 2; SO = 3; FL = 0; MISC = 0
            def make_head(h):
                hp = h % 2
                ve("dve", "tensor_scalar", [b_NC, b_offs], [b_biasT[hp]], biasT[hp][:, 0:nkt], NCt[:, 0:nkt, h],
                   offs[:, 16 * m, h:h + 1], None, ALU.subtract)
                tr(PS[MISC][0:16, 0:128], NCt[:, 16 * m:16 * m + 16, h], ident_f, [b_NC, b_const], [PB[MISC]])
                for i4 in range(4):
                    ve("dve", "tensor_scalar", [PB[MISC], b_offs, b_const], [b_crowb[hp]],
                       crowb[hp][:, i4 * 128:(i4 + 1) * 128], PS[MISC][0:16, 0:128],
                       offs[0:16, 16 * m, h:h + 1], negselc[:, i4:i4 + 1], ALU.subtract, ALU.mult)
                qf = big[:, h, :]; qs = big[:, H + h, :]
                kv = {}

                def load_kv(mx, kb):
                    if (mx, kb) in kv:
                        return kv[(mx, kb)]
                    i = kvctr[0] % NKV; kvctr[0] += 1
                    P.dma("sp", kbuf[i][:], KT_s[mx, h, :, kb * 512:(kb + 1) * 512], key=f"kb{i}",
                          reads=[b_KV], writes=[b_kbuf[i]])
                    P.dma("sp", vbuf[i][:], V_s[mx, kb * 512:(kb + 1) * 512, h * 128:(h + 1) * 128]
                          .rearrange("(k p) d -> p k d", p=128), key=f"vb{i}", reads=[b_KV], writes=[b_vbuf[i]])
                    kv.clear()
                    kv[(mx, kb)] = i
                    return i

                npair = nkt // 2
                fst = {}
                sst = {}
                cur = {}

                def fox_A(t):
                    if t % 4 == 0:
                        cur["f"] = load_kv(0, t // 4)
                    ikv = cur["f"]
                    bS = FS[t % 2]
                    dg = t >= 16 * m
                    mm(PS[bS][:, :], kbuf[ikv][:, (t % 4) * 128:(t % 4 + 1) * 128], qf, True, False,
                       [b_kbuf[ikv], b_big[h]], [PB[bS]])
                    mm(PS[bS][:, :], ones_bf[0:16, :], crowb[hp][:, :], False, not dg,
                       [b_const, b_crowb[hp]], [PB[bS]])
                    if dg:
                        mm(PS[bS][:, :], ident_bf, maskt[:, t - 16 * m, 1:513], False, True,
                           [b_const, b_mask], [PB[bS]])
                    ip = t % NR
                    act(PT[ip][:], PS[bS][:, :], AF.Exp, [PB[bS], b_biasT[hp]], [b_PT[ip]],
                        bias=biasT[hp][:, t:t + 1])
                    fst[t] = (ikv, t % 4, ip)

                def fox_B(t):
                    ikv, kk, ip = fst.pop(t)
                    mm(PS[FO][:, :], vbuf[ikv][:, kk, :], PT[ip][:], t == 0, t == nkt - 1,
                       [b_vbuf[ikv], b_PT[ip]], [PB[FO]])
                    if t == 0:
                        ve("dve", "tensor_copy", [b_PT[ip]], [b_Lacc], Lacc[:], PT[ip][:])
                    else:
                        ve("dve", "tensor_tensor", [b_PT[ip], b_Lacc], [b_Lacc], Lacc[:], Lacc[:], PT[ip][:],
                           ALU.add)

                def sb_A(ps):
                    JA = nkt - 1 - 2 * ps
                    if JA % 4 == 3:
                        cur["s"] = load_kv(1, JA // 4)
                    ikv = cur["s"]
                    z = ps % 2
                    for half, J in ((0, JA), (1, JA - 1)):
                        bZ = 4 + 2 * z + half
                        dg = J >= 16 * m
                        mm(PS[bZ][:, :], kbuf[ikv][:, (J % 4) * 128:(J % 4 + 1) * 128], qs, True, False,
                           [b_kbuf[ikv], b_big[H + h]], [PB[bZ]], skip_group_check=True)
                        if dg:
                            mm(PS[bZ][:, :], ident_bf, maskt[:, J - 16 * m, 0:512], False, False,
                               [b_const, b_mask], [PB[bZ]], skip_group_check=True)
                    sst[ps] = (ikv, JA % 4, z)

                def sb_exp(ps):
                    ikv, kkA, z = sst[ps]
                    pbz = [PB[4 + 2 * z], PB[5 + 2 * z]]
                    act(E1[:], PSZ[z][:, :], AF.Exp, pbz, [b_E1])

                def sb_ln(ps):
                    ikv, kkA, z = sst[ps]
                    act(SPb[z][:], E1[:], AF.Ln, [b_E1], [b_SPb[z]], bias=1.0)

                def sb_B(ps):
                    ikv, kkA, z = sst[ps]
                    bA = 4 + 2 * z; bB = 5 + 2 * z
                    prev = (ps - 1) % 2
                    mm(PS[bA][:, :], negtri, SPb[z][:, 0:512], False, ps == 0, [b_const, b_SPb[z]], [PB[bA]],
                       skip_group_check=True)
                    if ps >= 1:
                        mm(PS[bA][:, :], negones, SPa[prev][:], False, True, [b_const, b_SPa[prev]], [PB[bA]],
                           skip_group_check=True)
                    mm(PS[bB][:, :], negtri, SPb[z][:, 512:1024], False, False, [b_const, b_SPb[z]], [PB[bB]],
                       skip_group_check=True)
                    mm(PS[bB][:, :], negones, SPb[z][:, 0:512], False, ps == 0, [b_const, b_SPb[z]], [PB[bB]],
                       skip_group_check=True)
                    if ps >= 1:
                        mm(PS[bB][:, :], negones, SPa[prev][:], False, True, [b_const, b_SPa[prev]], [PB[bB]],
                           skip_group_check=True)
                    act(AT[z][:], PSZ[z][:, :], AF.Exp, [PB[bA], PB[bB]], [b_AT[z]])
                    if ps < npair - 1:
                        if ps == 0:
                            ve("pool", "tensor_tensor", [b_SPb[z]], [b_SPa[0]], SPa[0][:], SPb[z][:, 0:512],
                               SPb[z][:, 512:1024], ALU.add)
                        else:
                            ve("pool", "tensor_tensor", [b_SPb[z], b_SPa[prev]], [b_SPa[ps % 2]],
                               SPa[ps % 2][:], SPa[prev][:], SPb[z][:, 0:512], ALU.add)
                            ve("pool", "tensor_tensor", [b_SPb[z], b_SPa[ps % 2]], [b_SPa[ps % 2]],
                               SPa[ps % 2][:], SPa[ps % 2][:], SPb[z][:, 512:1024], ALU.add)

                def sb_C(ps):
                    ikv, kkA, z = sst.pop(ps)
                    mm(PS[SO][:, :], vbuf[ikv][:, kkA, :], AT[z][:, 0:512], ps == 0, False,
                       [b_vbuf[ikv], b_AT[z]], [PB[SO]])
                    mm(PS[SO][:, :], vbuf[ikv][:, kkA - 1, :], AT[z][:, 512:1024], False, ps == npair - 1,
                       [b_vbuf[ikv], b_AT[z]], [PB[SO]])

                def step(ps):
                    if ps < npair:
                        fox_A(2 * ps)
                        sb_A(ps)
                        sb_exp(ps)
                        fox_A(2 * ps + 1)
                    if 1 <= ps <= npair:
                        sb_B(ps - 1)
                    if ps < npair:
                        sb_ln(ps)
                    if 1 <= ps <= npair:
                        fox_B(2 * ps - 1)
                    if ps < npair:
                        fox_B(2 * ps)
                    if ps >= 2:
                        sb_C(ps - 2)
                def fin_f():
                    i = h % 4
                    mm(PS[FL][:, :], ones_f, Lacc[:], True, True, [b_const, b_Lacc], [PB[FL]])
                    ve("dve", "reciprocal", [PB[FL]], [b_tmpf[i]], tmpf[i][:], PS[FL][:, :])
                    ve("dve", "tensor_tensor", [PB[FO], b_tmpf[i]], [b_big[2 * H + h]], big[:, 2 * H + h, :],
                       PS[FO][:, :], tmpf[i][:], ALU.mult)

                def fin_s():
                    ve("dve", "tensor_copy", [PB[SO]], [b_big[3 * H + h]], big[:, 3 * H + h, :], PS[SO][:, :])

                return step, fin_f, fin_s

            npair_m = nkt // 2
            prev = None
            for h in range(H):
                hd = make_head(h)
                for ps in range(npair_m):
                    if prev is not None and ps < 2:
                        prev[0](npair_m + ps)
                        if ps == 0:
                            prev[1]()
                        else:
                            prev[2]()
                    hd[0](ps)
                prev = hd
            prev[0](npair_m); prev[1](); prev[0](npair_m + 1); prev[2]()
            for g in range(NGD):
                wA, bA = wload(wof_s[g], H)
                wB, bB = wload(wos_s[g], H)
                wGa, bGa = wload(wga_s[g], KC)
                wGb, bGb = wload(wgb_s[g], KC)
                for cc in range(GW // 128):
                    c = g * (GW // 128) + cc
                    cs = slice(cc * 128, (cc + 1) * 128)
                    byA = nbank(); byB = nbank(); bgA = nbank(); bgB = nbank()
                    for h in range(H):
                        mm(PS[byA][:, :], wA[:, h, cs], big[:, 2 * H + h, :], h == 0, h == H - 1,
                           [bA, b_big[2 * H + h]], [PB[byA]])
                    for h in range(H):
                        mm(PS[byB][:, :], wB[:, h, cs], big[:, 3 * H + h, :], h == 0, h == H - 1,
                           [bB, b_big[3 * H + h]], [PB[byB]])
                    for kc in range(KC):
                        mm(PS[bgA][:, :], wGa[:, kc, cs], xnT2[:, kc, :], kc == 0, kc == KC - 1,
                           [bGa, b_xn2[kc]], [PB[bgA]])
                    for kc in range(KC):
                        mm(PS[bgB][:, :], wGb[:, kc, cs], xnT2[:, kc, :], kc == 0, kc == KC - 1,
                           [bGb, b_xn2[kc]], [PB[bgB]])
                    act(tmpf[0][:], PS[bgA][:, :], AF.Sigmoid, [PB[bgA]], [b_tmpf[0]])
                    act(tmpf[1][:], PS[bgB][:, :], AF.Sigmoid, [PB[bgB]], [b_tmpf[1]])
                    ve("dve", "tensor_tensor", [PB[byA], b_tmpf[0]], [b_tmpf[2]], tmpf[2][:], PS[byA][:, :],
                       tmpf[0][:], ALU.mult)
                    ve("dve", "tensor_tensor", [PB[byB], b_tmpf[1]], [b_tmpf[3]], tmpf[3][:], PS[byB][:, :],
                       tmpf[1][:], ALU.mult)
                    ve("dve", "tensor_tensor", [b_tmpf[2], b_tmpf[3]], [b_big[c]], big[:, c, :], tmpf[2][:],
                       tmpf[3][:], ALU.add)
            for g in range(NGD):
                wb, bwb = wload(wo_s[g], KC)
                for cc in range(GW // 128):
                    c = g * (GW // 128) + cc
                    bk = nbank()
                    for kc in range(KC):
                        mm(PS[bk][:, :], wb[:, kc, cc * 128:(cc + 1) * 128], big[:, kc, :], kc == 0, kc == KC - 1,
                           [bwb, b_big[kc]], [PB[bk]])
                    ve("dve", "tensor_tensor", [PB[bk], b_xT[c]], [b_xT[c]], xT[:, c, :], xT[:, c, :], PS[bk][:, :],
                       ALU.add)
            rmsnorm_fm(gmlp, xnT2, b_xn2)
            for half in range(2):
                for g in range(HJ * 128 // GW):
                    wb, bwb = wload(wup_s[half * (HJ * 128 // GW) + g], KC)
                    for jj in range(GW // 128):
                        j = g * (GW // 128) + jj
                        bk = nbank()
                        for kc in range(KC):
                            mm(PS[bk][:, :], wb[:, kc, jj * 128:(jj + 1) * 128], xnT2[:, kc, :], kc == 0,
                               kc == KC - 1, [bwb, b_xn2[kc]], [PB[bk]])
                        i = j % 4
                        act(tmpf[i][:], PS[bk][:, :], AF.Square, [PB[bk]], [b_tmpf[i]])
                        ve("dve", "scalar_tensor_tensor", [PB[bk], b_tmpf[i]], [b_big[j]], big[:, j, :],
                           PS[bk][:, :], 0.0, tmpf[i][:], ALU.is_gt, ALU.mult)
                for g in range(NGD):
                    pieces = []
                    step_k = min(KC, HJ)
                    for k0 in range(0, HJ, step_k):
                        wb, bwb = wload(wdn_s[g, :, half * HJ + k0: half * HJ + k0 + step_k, :], step_k)
                        pieces.append((wb, bwb, k0, step_k))
                    for cc in range(GW // 128):
                        c = g * (GW // 128) + cc
                        bk = nbank()
                        n = 0
                        for (wb, bwb, k0, sk) in pieces:
                            for kk in range(sk):
                                mm(PS[bk][:, :], wb[:, kk, cc * 128:(cc + 1) * 128], big[:, k0 + kk, :], n == 0,
                                   n == HJ - 1, [bwb, b_big[k0 + kk]], [PB[bk]])
                                n += 1
                        ve("dve", "tensor_tensor", [PB[bk], b_xT[c]], [b_xT[c]], xT[:, c, :], xT[:, c, :],
                           PS[bk][:, :], ALU.add)
            rmsnorm_fm(gfin, xT, b_xT)
            for sub in range(4):
                i = tctr[0] % 2; tctr[0] += 1
                for k0 in range(0, KC, 4):
                    bk = nbank()
                    kn = min(4, KC - k0)
                    for q in range(kn):
                        tr(PS[bk][:, q * 128:(q + 1) * 128], xT[:, k0 + q, sub * 128:(sub + 1) * 128], ident_f,
                           [b_xT[k0 + q], b_const], [PB[bk]])
                    act(stage[i][:, k0 * 128:(k0 + kn) * 128], PS[bk][:, 0:kn * 128], AF.Copy, [PB[bk]],
                        [b_stage[i]])
                r0 = m * 512 + sub * 128
                P.dma("pool", out_d[r0:r0 + 128, :], stage[i][:], key=f"out{i}", reads=[b_stage[i]],
                      writes=[P.buf()])
        P.barrier()
        P.emit()
    return nc, P


def host_consts(r):
    cbf = np.zeros((128, 4, 128), np.float32)
    cbf[:, 0, :] = np.eye(128)
    j = np.arange(128)[:, None]; s = np.arange(128)[None, :]
    cbf[:, 1, :] = -(j >= s).astype(np.float32)
    cbf[:, 2, :] = -1.0
    cbf[:, 3, :] = 1.0
    cf32 = np.zeros((128, 3, 128), np.float32)
    cf32[:, 0, :] = np.eye(128)
    cf32[:, 1, :] = (j <= s).astype(np.float32)
    cf32[:, 2, :] = 1.0
    sel = np.zeros((16, 4), np.float32)
    for i in range(4):
        sel[4 * r + i, i] = -1.0
    sidx = np.arange(128)[:, None, None]
    idx = np.arange(16)[None, :, None]
    col = np.arange(513)[None, None, :]
    keypos = 128 * idx + sidx
    qpos = 512 * r + col - 1
    masks = np.where(keypos > qpos, NEG, 0.0).astype(np.float32)
    bf = ml_dtypes.bfloat16
    return cbf.astype(bf), cf32, sel, masks.astype(bf)


_CACHE = {}


def run(cfg, x, norm_mix_g, w_in, b_forget, w_out_fox, w_out_sb, w_out, norm_mlp_g, w_mlp_up, w_mlp_down,
        norm_final_g):
    D = cfg["D"]; H = cfg["H"]; NSLOT = cfg["NSLOT"]; KC = D // 128
    S = 2048 * NSLOT
    key = tuple(sorted(cfg.items()))
    if key not in _CACHE:
        _CACHE[key] = build(cfg)
    nc, P = _CACHE[key]
    f = lambda a: np.ascontiguousarray(np.asarray(a, dtype=np.float32))
    x = f(x)

    def col(g):
        return np.ascontiguousarray(f(g).reshape(KC, 128).T)

    in_maps = []
    for c in range(8):
        b = c // 4; r = c % 4
        cbf, cf32, sel, masks = host_consts(r)
        xb = x[b]
        own = np.ascontiguousarray(xb.reshape(NSLOT, 4, 512, D)[:, r].reshape(NSLOT * 512, D))
        in_maps.append(dict(
            x_seq=xb, x_own=own, w_in=f(w_in[0]), w_of=f(w_out_fox[0]), w_os=f(w_out_sb[0]), w_o=f(w_out[0]),
            w_up=f(w_mlp_up[0]), w_dn=f(w_mlp_down[0]), gmix=col(norm_mix_g[0]), gmlp=col(norm_mlp_g[0]),
            gfin=col(norm_final_g), bfg=np.ascontiguousarray(np.broadcast_to(f(b_forget[0])[None, :], (128, H))),
            masks=masks, sel=sel, cbf=cbf, cf32=cf32))
    res = run_bass_kernel_spmd(nc, in_maps, core_ids=list(range(8)))
    out = np.empty((2, S, D), np.float32)
    for c in range(8):
        b = c // 4; r = c % 4
        o = np.asarray(res.results[c]["out"]).reshape(NSLOT, 512, D)
        out[b].reshape(NSLOT, 4, 512, D)[:, r] = o
    return out


def kernel(**inputs):
    return run(FULL, **inputs)
```

```python
import contextlib
import numpy as np
import ml_dtypes
import concourse.bass as bass
import concourse.mybir as mybir
from concourse.bass_utils import run_bass_kernel_spmd

F32 = mybir.dt.float32
BF16 = mybir.dt.bfloat16
AF = mybir.ActivationFunctionType
ALU = mybir.AluOpType

ENGS = ("pe", "act", "dve", "pool", "sp")
EPOCH = 20000
DMA_EPOCH = 1500
NEG = -30000.0
RMS_EPS = 1e-6


class Buf:
    __slots__ = ("name", "writers", "readers")

    def __init__(self, name):
        self.name = name
        self.writers = []
        self.readers = []


class Op:
    __slots__ = ("eng", "fn", "is_dma", "key", "deps", "flag", "cnt", "sem", "barrier")

    def __init__(self, eng, fn, is_dma=False, key=None):
        self.eng = eng
        self.fn = fn
        self.is_dma = is_dma
        self.key = key
        self.deps = []
        self.flag = False
        self.cnt = None
        self.sem = None
        self.barrier = None


class Prog:
    def __init__(self, nc):
        self.nc = nc
        self.ops = {e: [] for e in ENGS}
        self.dma_keys = {}
        self.nbuf = 0
        self.n_ops = 0

    def buf(self, name=None):
        self.nbuf += 1
        return Buf(name or f"b{self.nbuf}")

    def bufs(self, n, name="b"):
        return [self.buf(f"{name}{i}") for i in range(n)]

    def _track(self, op, reads, writes):
        deps = []
        for b in reads:
            deps.extend(b.writers)
            b.readers.append(op)
        for b in writes:
            if b.readers:
                deps.extend(b.readers)
                deps.extend(b.writers)
                b.writers = [op]
                b.readers = []
            elif b.writers and not all(w.eng == op.eng for w in b.writers):
                deps.extend(b.writers)
                b.writers = [op]
            elif op.is_dma or any(w.is_dma for w in b.writers):
                b.writers.append(op)
            else:
                b.writers = [op]
        out = []
        seen = set()
        for d in deps:
            if d is op or id(d) in seen:
                continue
            seen.add(id(d))
            out.append(d)
        return out

    def op(self, eng, fn, reads=(), writes=()):
        o = Op(eng, fn)
        raw = set()
        for b in reads:
            for w in b.writers:
                raw.add(id(w))
        deps = self._track(o, reads, writes)
        keep = []
        for d in deps:
            if (not d.is_dma) and d.eng == eng:
                if eng == "pe" or id(d) not in raw:
                    continue
            keep.append(d)
            d.flag = True
        o.deps = keep
        self.ops[eng].append(o)
        self.n_ops += 1
        return o

    def dma(self, eng, out, in_, key, reads=(), writes=(), **kw):
        def fn(e):
            return e.dma_start(out=out, in_=in_, **kw)
        o = Op(eng, fn, is_dma=True, key=key)
        o.deps = self._track(o, reads, writes)
        for d in o.deps:
            d.flag = True
        self.ops[eng].append(o)
        self.dma_keys.setdefault(key, []).append(o)
        self.n_ops += 1
        return o

    def barrier(self):
        targets = []
        for e in ENGS:
            for o in reversed(self.ops[e]):
                if (not o.is_dma) and o.barrier is None:
                    o.flag = True
                    targets.append(o)
                    break
        for lst in self.dma_keys.values():
            if lst:
                targets.append(lst[-1])
        for e in ENGS:
            o = Op(e, None)
            o.barrier = list(targets)
            self.ops[e].append(o)

    def emit(self):
        nc = self.nc
        sems = {}

        def get_sem(name):
            if name not in sems:
                sems[name] = nc.alloc_semaphore(name=name)
            return sems[name]

        for e in ENGS:
            c = 0
            ep = 0
            for o in self.ops[e]:
                if o.is_dma or o.barrier is not None:
                    continue
                if o.flag:
                    c += 1
                    if c > EPOCH:
                        ep += 1
                        c = 1
                    o.sem = f"c_{e}_{ep}"
                    o.cnt = c
        for key, lst in self.dma_keys.items():
            c = 0
            ep = 0
            for o in lst:
                c += 1
                if c > DMA_EPOCH:
                    ep += 1
                    c = 1
                o.sem = f"d_{key}_{ep}"
                o.cnt = 16 * c

        def emit_engine(ename, e):
            waited = {}
            for o in self.ops[ename]:
                deps = o.barrier if o.barrier is not None else o.deps
                need = {}
                for d in deps:
                    if d.sem is None:
                        continue
                    if need.get(d.sem, 0) < d.cnt:
                        need[d.sem] = d.cnt
                for s, v in need.items():
                    if waited.get(s, 0) >= v:
                        continue
                    e.wait_ge(get_sem(s), v)
                    waited[s] = v
                if o.barrier is not None:
                    continue
                inst = o.fn(e)
                if o.is_dma:
                    inst.then_inc(get_sem(o.sem), 16)
                elif o.flag:
                    inst.then_inc(get_sem(o.sem), 1)

        with nc.Block() as block:
            @block.tensor
            def _(e):
                emit_engine("pe", e)

            @block.scalar
            def _(e):
                emit_engine("act", e)

            @block.vector
            def _(e):
                emit_engine("dve", e)

            @block.gpsimd
            def _(e):
                emit_engine("pool", e)

            @block.sync
            def _(e):
                emit_engine("sp", e)
        self.n_sems = len(sems)


FULL = dict(D=2048, H=8, DFF=8192, NSLOT=8)


def build(cfg):
    D = cfg["D"]; H = cfg["H"]; DFF = cfg["DFF"]; NSLOT = cfg["NSLOT"]
    KC = D // 128; W = H * 128; JC = DFF // 128
    S = 2048 * NSLOT; NB = S // 512; NT = S // 128; NOWN = 512 * NSLOT
    GW = 256
    HJ = JC // 2
    assert HJ <= 32 and HJ % KC == 0 or HJ <= KC
    IN_COLS = 6 * W + H + 2 * D
    oqf = 0; okf = W; ovf = 2 * W; off_ = 3 * W
    oqs = 3 * W + H; oks = oqs + W; ovs = oks + W; oga = ovs + W; ogb = oga + D
    SCALE = 128 ** -0.5

    nc = bass.Bass("TRN2", target_bir_lowering=False)
    P = Prog(nc)

    def din(name, shape, dt=F32):
        return nc.dram_tensor(name, list(shape), dt, kind="ExternalInput").ap()

    def dscr(name, shape, dt=BF16):
        return nc.dram_tensor(name, list(shape), dt, kind="Internal").ap()

    x_seq = din("x_seq", [S, D]); x_own = din("x_own", [NOWN, D])
    w_in = din("w_in", [D, IN_COLS]); w_of = din("w_of", [W, D]); w_os = din("w_os", [W, D])
    w_o = din("w_o", [D, D]); w_up = din("w_up", [D, DFF]); w_dn = din("w_dn", [DFF, D])
    gmix_d = din("gmix", [128, KC]); gmlp_d = din("gmlp", [128, KC]); gfin_d = din("gfin", [128, KC])
    bfg_d = din("bfg", [128, H])
    masks_d = din("masks", [128, 16, 513], BF16)
    sel_d = din("sel", [16, 4], F32)
    cbf_d = din("cbf", [128, 4, 128], BF16)
    cf32_d = din("cf32", [128, 3, 128], F32)
    out_d = nc.dram_tensor("out", [NOWN, D], F32, kind="ExternalOutput").ap()

    NGQ = 2 * W // GW; NGD = D // GW; NGU = DFF // GW
    wq_s = dscr("wq_s", [NGQ, 128, KC, GW]); wga_s = dscr("wga_s", [NGD, 128, KC, GW])
    wgb_s = dscr("wgb_s", [NGD, 128, KC, GW]); wof_s = dscr("wof_s", [NGD, 128, H, GW])
    wos_s = dscr("wos_s", [NGD, 128, H, GW]); wo_s = dscr("wo_s", [NGD, 128, KC, GW])
    wup_s = dscr("wup_s", [NGU, 128, KC, GW]); wdn_s = dscr("wdn_s", [NGD, 128, JC, GW])
    KT_s = dscr("KT_s", [2, H, 128, S]); V_s = dscr("V_s", [2, S, W])

    es = contextlib.ExitStack()

    def sb(name, shape, dt, stack=None):
        return (stack or es).enter_context(nc.sbuf_tensor("s_" + name, list(shape), dt))

    with es:
        PSZ = [es.enter_context(nc.psum_tensor(f"psz{i}", [128, 1024], F32)) for i in range(2)]
        PSa = [es.enter_context(nc.psum_tensor(f"ps{i}", [128, 512], F32)) for i in range(4)]
        PS = [p[:, :] for p in PSa] + [PSZ[0][:, 0:512], PSZ[0][:, 512:1024], PSZ[1][:, 0:512], PSZ[1][:, 512:1024]]
        PB = P.bufs(8, "psb")
        rr = [0]

        def nbank():
            i = rr[0] % 8
            rr[0] += 1
            return i

        cbf = sb("cbf", [128, 4, 128], BF16); cf32 = sb("cf32", [128, 3, 128], F32)
        negselc = sb("negselc", [16, 4], F32)
        gmix = sb("gmix", [128, KC], F32); gmlp = sb("gmlp", [128, KC], F32); gfin = sb("gfin", [128, KC], F32)
        bfg = sb("bfg", [128, H], F32)
        NCt = sb("NCt", [128, NT, H], F32); offs = sb("offs", [128, NT, H], F32)
        fl = sb("fl", [128, NT, H], F32)
        b_const = P.buf("const"); b_NC = P.buf("NC"); b_offs = P.buf("offs"); b_fl = P.buf("fl")
        ident_bf = cbf[:, 0, :]; negtri = cbf[:, 1, :]; negones = cbf[:, 2, :]; ones_bf = cbf[:, 3, :]
        ident_f = cf32[:, 0, :]; tri_f = cf32[:, 1, :]; ones_f = cf32[:, 2, :]

        for dst, src in ((cbf, cbf_d), (cf32, cf32_d), (negselc, sel_d), (gmix, gmix_d), (gmlp, gmlp_d),
                         (gfin, gfin_d), (bfg, bfg_d)):
            P.dma("sp", dst[:], src, key="const", writes=[b_const])

        def mm(out, lhsT, rhs, start, stop, reads, writes, **kw):
            P.op("pe", lambda e: e.matmul(out, lhsT, rhs, start=start, stop=stop, **kw), reads, writes)

        def tr(out, in_, ident, reads, writes):
            P.op("pe", lambda e: e.transpose(out, in_, ident), reads, writes)

        def act(out, in_, func, reads, writes, **kw):
            P.op("act", lambda e: e.activation(out, in_, func, **kw), reads, writes)

        def ve(eng, name, reads, writes, *a, **kw):
            P.op(eng, lambda e: getattr(e, name)(*a, **kw), reads, writes)

        b_wscr = P.buf("wscr")

        pending_casts = []

        def flush_casts(n):
            for _ in range(min(n, len(pending_casts))):
                d_, s_ = pending_casts.pop(0)
                P.dma("pool", d_, s_, key="cast", writes=[b_wscr])

        def cast_groups(w_ap, r0, nrows, c0, ncols, dst, gbase):
            kcx = nrows // 128
            step = 16 if kcx > 16 else kcx
            for g in range(ncols // GW):
                for k0 in range(0, kcx, step):
                    src = w_ap[r0 + k0 * 128: r0 + (k0 + step) * 128, c0 + g * GW: c0 + (g + 1) * GW]
                    src = src.rearrange("(kc f) c -> f kc c", f=128)
                    pending_casts.append((dst[gbase + g, :, k0:k0 + step, :], src))

        with contextlib.ExitStack() as s1:
            wkv = sb("wkv", [128, KC, 2 * W + 8], BF16, s1)
            xst = [sb(f"xst{i}", [128, D], F32, s1) for i in range(4)]
            xs = [sb(f"xs{i}", [128, D], BF16, s1) for i in range(4)]
            junk = sb("junk", [128, D], BF16, s1)
            ssq = sb("ssq", [128, 8], F32, s1); rstd = sb("rstd", [128, 8], F32, s1)
            xnT = [sb(f"xnT{i}", [128, KC, 512], BF16, s1) for i in range(2)]
            kst = [sb(f"kst{i}", [128, H, 512], BF16, s1) for i in range(2)]
            vst = [sb(f"vst{i}", [128, 4, W], BF16, s1) for i in range(2)]
            b_wkv = P.buf("wkv"); b_xst = P.bufs(4, "xst"); b_xs = P.bufs(4, "xs"); b_junk = P.buf("junk")
            b_ssq = P.bufs(8, "ssq"); b_rstd = P.bufs(8, "rstd"); b_xnT = P.bufs(2, "xnT")
            b_kst = P.bufs(2, "kst"); b_vst = P.bufs(2, "vst"); b_KV = P.buf("KVs")

            cnt = [0]

            stg = [0]

            def load_wkv(mx):
                ok_, ov_ = (okf, ovf) if mx == 0 else (oks, ovs)
                ks_ = D // W
                for (c0, dcol) in ((ok_, 0), (ov_, W)):
                    for k0 in range(0, KC, ks_):
                        i = stg[0] % 4; stg[0] += 1
                        src = w_in[k0 * 128:(k0 + ks_) * 128, c0:c0 + W].rearrange("(kc f) c -> f kc c", f=128)
                        view = xst[i][:].rearrange("p (k c) -> p k c", k=ks_)
                        P.dma("sp", view, src, key=f"xst{i}", writes=[b_xst[i]])
                        ve("dve", "tensor_copy", [b_xst[i]], [b_wkv], wkv[:, k0:k0 + ks_, dcol:dcol + W], view)
                if mx == 0:
                    src = w_in[:, off_:off_ + H].rearrange("(kc f) c -> f kc c", f=128)
                    P.dma("pool", wkv[:, :, 2 * W:2 * W + H], src, key="wkv", writes=[b_wkv])

            def pre(gb):
                blk = gb % NB
                for sub in range(4):
                    i = cnt[0] % 4; j = cnt[0] % 8; cnt[0] += 1
                    r0 = blk * 512 + sub * 128
                    P.dma("sp", xst[i][:], x_seq[r0:r0 + 128, :], key=f"xst{i}", writes=[b_xst[i]])
                    act(junk[:], xst[i][:], AF.Square, [b_xst[i]], [b_junk, b_ssq[j]],
                        accum_out=ssq[:, j:j + 1])
                    ve("dve", "tensor_scalar", [b_ssq[j]], [b_rstd[j]], rstd[:, j:j + 1], ssq[:, j:j + 1],
                       1.0 / D, RMS_EPS, ALU.mult, ALU.add)
                    act(rstd[:, j:j + 1], rstd[:, j:j + 1], AF.Sqrt, [b_rstd[j]], [b_rstd[j]])
                    ve("dve", "reciprocal", [b_rstd[j]], [b_rstd[j]], rstd[:, j:j + 1], rstd[:, j:j + 1])
                    ve("dve", "tensor_scalar", [b_xst[i], b_rstd[j]], [b_xs[sub]], xs[sub][:], xst[i][:],
                       rstd[:, j:j + 1], None, ALU.mult)

            def trans(gb):
                xn = xnT[gb % 2]; bxn = b_xnT[gb % 2]
                for kc in range(KC):
                    bk = nbank()
                    for sub in range(4):
                        mm(PS[bk][:, sub * 128:(sub + 1) * 128], xs[sub][:, kc * 128:(kc + 1) * 128],
                           ident_bf, True, True, [b_xs[sub], b_const], [PB[bk]], skip_group_check=True)
                    act(xn[:, kc, :], PS[bk][:, 0:512], AF.Copy, [PB[bk], b_const], [bxn],
                        scale=gmix[:, kc:kc + 1])

            def kproj(gb):
                mx = gb // NB; blk = gb % NB
                xn = xnT[gb % 2]; bxn = b_xnT[gb % 2]
                ks = kst[gb % 2]; bks = b_kst[gb % 2]
                for h in range(H):
                    bk = nbank()
                    for kc in range(KC):
                        mm(PS[bk][:, :], wkv[:, kc, h * 128:(h + 1) * 128], xn[:, kc, :], kc == 0, kc == KC - 1,
                           [b_wkv, bxn], [PB[bk]])
                    ve("dve", "tensor_copy", [PB[bk]], [bks], ks[:, h, :], PS[bk][:, :])
                P.dma("pool", KT_s[mx, :, :, blk * 512:(blk + 1) * 512].rearrange("h d t -> d h t"), ks[:],
                      key=f"kst{gb % 2}", reads=[bks], writes=[b_KV])

            def vproj(gb):
                mx = gb // NB; blk = gb % NB
                xn = xnT[gb % 2]; bxn = b_xnT[gb % 2]
                vs = vst[gb % 2]; bvs = b_vst[gb % 2]
                for sub in range(4):
                    for cg in range(W // 512):
                        bk = nbank()
                        for kc in range(KC):
                            mm(PS[bk][:, :], xn[:, kc, sub * 128:(sub + 1) * 128],
                               wkv[:, kc, W + cg * 512: W + (cg + 1) * 512], kc == 0, kc == KC - 1,
                               [b_wkv, bxn], [PB[bk]])
                        if (sub + cg) % 2 == 0:
                            act(vs[:, sub, cg * 512:(cg + 1) * 512], PS[bk][:, :], AF.Copy, [PB[bk]], [bvs])
                        else:
                            ve("dve", "tensor_copy", [PB[bk]], [bvs], vs[:, sub, cg * 512:(cg + 1) * 512],
                               PS[bk][:, :])
                P.dma("pool", V_s[mx, blk * 512:(blk + 1) * 512, :].rearrange("(s p) c -> p s c", p=128),
                      vs[:], key=f"vst{gb % 2}", reads=[bvs], writes=[b_KV])
                if mx == 0:
                    for sub in range(4):
                        bk = nbank()
                        for kc in range(KC):
                            mm(PS[bk][:, 0:H], xn[:, kc, sub * 128:(sub + 1) * 128],
                               wkv[:, kc, 2 * W:2 * W + H], kc == 0, kc == KC - 1, [b_wkv, bxn], [PB[bk]])
                        ve("dve", "tensor_tensor", [PB[bk], b_const], [b_fl], fl[:, blk * 4 + sub, :],
                           PS[bk][:, 0:H], bfg[:, :], ALU.add)

            load_wkv(0)
            pre(0)
            for i in range(2):
                cast_groups(w_in, 0, D, (oqf, oqs)[i], W, wq_s, i * (W // GW))
            cast_groups(w_in, 0, D, oga, D, wga_s, 0)
            cast_groups(w_in, 0, D, ogb, D, wgb_s, 0)
            cast_groups(w_of, 0, W, 0, D, wof_s, 0)
            cast_groups(w_os, 0, W, 0, D, wos_s, 0)
            cast_groups(w_o, 0, D, 0, D, wo_s, 0)
            cast_groups(w_up, 0, D, 0, DFF, wup_s, 0)
            cast_groups(w_dn, 0, DFF, 0, D, wdn_s, 0)
            trans(0)
            for gb in range(2 * NB):
                if gb == NB:
                    load_wkv(1)
                if gb + 1 < 2 * NB:
                    pre(gb + 1)
                kproj(gb)
                if gb + 1 < 2 * NB:
                    trans(gb + 1)
                vproj(gb)
                flush_casts(2 if gb + 1 < 2 * NB else len(pending_casts))
        P.barrier()

        NF = NT * H
        with contextlib.ExitStack() as s2:
            e1 = sb("e1", [128, NF], F32, s2); l1 = sb("l1", [128, NF], F32, s2)
            cw = sb("cw", [128, NT, H], F32, s2)
            sa = sb("sa", [128, NT, H], F32, s2); sbb = sb("sbb", [128, NT, H], F32, s2)
            b_e1 = P.buf(); b_l1 = P.buf(); b_cw = P.buf(); b_sa = P.buf(); b_sb = P.buf()
            flf = fl[:].rearrange("p j h -> p (j h)")
            act(e1[:], flf, AF.Exp, [b_fl], [b_e1], scale=-1.0)
            act(l1[:], e1[:], AF.Ln, [b_e1], [b_l1], bias=1.0)
            cwf = cw[:].rearrange("p j h -> p (j h)"); saf = sa[:].rearrange("p j h -> p (j h)")
            for c0 in range(0, NF, 512):
                c1 = min(NF, c0 + 512)
                bk = nbank()
                mm(PS[bk][:, 0:c1 - c0], tri_f, l1[:, c0:c1], True, True, [b_const, b_l1], [PB[bk]])
                ve("dve", "tensor_copy", [PB[bk]], [b_cw], cwf[:, c0:c1], PS[bk][:, 0:c1 - c0])
                bk = nbank()
                mm(PS[bk][:, 0:c1 - c0], ones_f, l1[:, c0:c1], True, True, [b_const, b_l1], [PB[bk]])
                ve("dve", "tensor_copy", [PB[bk]], [b_sa], saf[:, c0:c1], PS[bk][:, 0:c1 - c0])
            a, b_, ba, bb = sa, sbb, b_sa, b_sb
            d = 1
            while d < NT:
                ve("dve", "tensor_tensor", [ba], [bb], b_[:, d:, :], a[:, d:, :], a[:, :NT - d, :], ALU.add)
                ve("dve", "tensor_copy", [ba], [bb], b_[:, :d, :], a[:, :d, :])
                a, b_, ba, bb = b_, a, bb, ba
                d *= 2
            ve("dve", "memset", [], [b_offs], offs[:, 0:1, :], 0.0)
            ve("dve", "tensor_copy", [ba], [b_offs], offs[:, 1:, :], a[:, :NT - 1, :])
            ve("dve", "tensor_tensor", [b_cw, b_offs], [b_NC], NCt[:], cw[:], offs[:], ALU.add)
        P.barrier()

        xT = sb("xT", [128, KC, 512], F32); b_xT = P.bufs(KC, "xT")
        xnT2 = sb("xnT2", [128, KC, 512], BF16); b_xn2 = P.bufs(KC, "xn2")
        big = sb("big", [128, 32, 512], BF16); b_big = P.bufs(32, "big")
        maskt = sb("maskt", [128, 16, 513], BF16); b_mask = P.buf("mask")
        stage = [sb(f"stage{i}", [128, D], F32) for i in range(2)]; b_stage = P.bufs(2, "stage")
        NW = 4
        wbuf = [sb(f"wbuf{i}", [128, KC, GW], BF16) for i in range(NW)]; b_wbuf = P.bufs(NW, "wbuf")
        NKV = 4
        kbuf = [sb(f"kbuf{i}", [128, 512], BF16) for i in range(NKV)]; b_kbuf = P.bufs(NKV, "kbuf")
        vbuf = [sb(f"vbuf{i}", [128, 4, 128], BF16) for i in range(NKV)]; b_vbuf = P.bufs(NKV, "vbuf")
        NR = 3
        PT = [sb(f"PT{i}", [128, 512], BF16) for i in range(NR)]; b_PT = P.bufs(NR, "PT")
        AT = [sb(f"AT{i}", [128, 1024], BF16) for i in range(2)]; b_AT = P.bufs(2, "AT")
        SPb = [sb(f"SPb{i}", [128, 1024], BF16) for i in range(2)]; b_SPb = P.bufs(2, "SPb")
        E1 = sb("E1", [128, 1024], F32); b_E1 = P.buf("E1")
        SPa = [sb(f"SPa{i}", [128, 512], BF16) for i in range(2)]; b_SPa = P.bufs(2, "SPa")
        biasT = [sb(f"biasT{i}", [128, NT], F32) for i in range(2)]; b_biasT = P.bufs(2, "biasT")
        crowb = [sb(f"crowb{i}", [16, 512], BF16) for i in range(2)]; b_crowb = P.bufs(2, "crowb")
        sq = [sb(f"sq{i}", [128, 512], BF16) for i in range(4)]; b_sq = P.bufs(4, "sq")
        rrep = sb("rrep", [128, 512], F32); b_rrep = P.buf("rrep")
        Lacc = sb("Lacc", [128, 512], F32); b_Lacc = P.buf("Lacc")
        tmpf = [sb(f"tmpf{i}", [128, 512], F32) for i in range(4)]; b_tmpf = P.bufs(4, "tmpf")

        P.dma("sp", maskt[:], masks_d, key="mask", writes=[b_mask])

        wctr = [0]

        def wload(src_ap, kcx):
            i = wctr[0] % NW
            wctr[0] += 1
            P.dma("sp", wbuf[i][:, 0:kcx, :], src_ap, key=f"w{i}", reads=[b_wscr], writes=[b_wbuf[i]])
            return wbuf[i], b_wbuf[i]

        def rmsnorm_fm(g_col, dst, b_dst, in_place=False):
            bk = nbank()
            for kc in range(KC):
                i = kc % 4
                act(sq[i][:], xT[:, kc, :], AF.Square, [b_xT[kc]], [b_sq[i]])
                mm(PS[bk][:, :], ones_bf, sq[i][:], kc == 0, kc == KC - 1, [b_const, b_sq[i]], [PB[bk]])
            ve("dve", "tensor_scalar", [PB[bk]], [b_rrep], rrep[:], PS[bk][:, :], 1.0 / D, RMS_EPS, ALU.mult, ALU.add)
            act(rrep[:], rrep[:], AF.Sqrt, [b_rrep], [b_rrep])
            ve("dve", "reciprocal", [b_rrep], [b_rrep], rrep[:], rrep[:])
            for kc in range(KC):
                ve("dve", "scalar_tensor_tensor", [b_xT[kc], b_rrep, b_const], [b_dst[kc]],
                   dst[:, kc, :], xT[:, kc, :], g_col[:, kc:kc + 1], rrep[:], ALU.mult, ALU.mult)

        kvctr = [0]
        tctr = [0]

        for m in range(NSLOT):
            for sub in range(4):
                i = tctr[0] % 2; tctr[0] += 1
                r0 = m * 512 + sub * 128
                P.dma("sp", stage[i][:], x_own[r0:r0 + 128, :], key=f"stage{i}", writes=[b_stage[i]])
                for k0 in range(0, KC, 4):
                    bk = nbank()
                    kn = min(4, KC - k0)
                    for q in range(kn):
                        tr(PS[bk][:, q * 128:(q + 1) * 128], stage[i][:, (k0 + q) * 128:(k0 + q + 1) * 128],
                           ident_f, [b_stage[i], b_const], [PB[bk]])
                    ve("dve", "tensor_copy", [PB[bk]], b_xT[k0:k0 + kn],
                       xT[:, k0:k0 + kn, sub * 128:(sub + 1) * 128],
                       PS[bk][:, 0:kn * 128].rearrange("p (k t) -> p k t", t=128))
            rmsnorm_fm(gmix, xnT2, b_xn2)
            for g in range(NGQ):
                wb, bwb = wload(wq_s[g], KC)
                for hh in range(GW // 128):
                    hd = g * (GW // 128) + hh
                    bk = nbank()
                    for kc in range(KC):
                        mm(PS[bk][:, :], wb[:, kc, hh * 128:(hh + 1) * 128], xnT2[:, kc, :], kc == 0, kc == KC - 1,
                           [bwb, b_xn2[kc]], [PB[bk]])
                    act(big[:, hd, :], PS[bk][:, :], AF.Copy, [PB[bk]], [b_big[hd]], scale=SCALE)
            nkt = 16 * m + 16
            FS = (0, 1); FO = 2; SO = 3; FL = 0; MISC = 0
            def make_head(h):
                hp = h % 2
                ve("dve", "tensor_scalar", [b_NC, b_offs], [b_biasT[hp]], biasT[hp][:, 0:nkt], NCt[:, 0:nkt, h],
                   offs[:, 16 * m, h:h + 1], None, ALU.subtract)
                tr(PS[MISC][0:16, 0:128], NCt[:, 16 * m:16 * m + 16, h], ident_f, [b_NC, b_const], [PB[MISC]])
                for i4 in range(4):
                    ve("dve", "tensor_scalar", [PB[MISC], b_offs, b_const], [b_crowb[hp]],
                       crowb[hp][:, i4 * 128:(i4 + 1) * 128], PS[MISC][0:16, 0:128],
                       offs[0:16, 16 * m, h:h + 1], negselc[:, i4:i4 + 1], ALU.subtract, ALU.mult)
                qf = big[:, h, :]; qs = big[:, H + h, :]
                kv = {}

                def load_kv(mx, kb):
                    if (mx, kb) in kv:
                        return kv[(mx, kb)]
                    i = kvctr[0] % NKV; kvctr[0] += 1
                    P.dma("sp", kbuf[i][:], KT_s[mx, h, :, kb * 512:(kb + 1) * 512], key=f"kb{i}",
                          reads=[b_KV], writes=[b_kbuf[i]])
                    P.dma("sp", vbuf[i][:], V_s[mx, kb * 512:(kb + 1) * 512, h * 128:(h + 1) * 128]
                          .rearrange("(k p) d -> p k d", p=128), key=f"vb{i}", reads=[b_KV], writes=[b_vbuf[i]])
                    kv.clear()
                    kv[(mx, kb)] = i
                    return i

                npair = nkt // 2
                fst = {}
                sst = {}
                cur = {}

                def fox_A(t):
                    if t % 4 == 0:
                        cur["f"] = load_kv(0, t // 4)
                    ikv = cur["f"]
                    bS = FS[t % 2]
                    dg = t >= 16 * m
                    mm(PS[bS][:, :], kbuf[ikv][:, (t % 4) * 128:(t % 4 + 1) * 128], qf, True, False,
                       [b_kbuf[ikv], b_big[h]], [PB[bS]])
                    mm(PS[bS][:, :], ones_bf[0:16, :], crowb[hp][:, :], False, not dg,
                       [b_const, b_crowb[hp]], [PB[bS]])
                    if dg:
                        mm(PS[bS][:, :], ident_bf, maskt[:, t - 16 * m, 1:513], False, True,
                           [b_const, b_mask], [PB[bS]])
                    ip = t % NR
                    act(PT[ip][:], PS[bS][:, :], AF.Exp, [PB[bS], b_biasT[hp]], [b_PT[ip]],
                        bias=biasT[hp][:, t:t + 1])
                    fst[t] = (ikv, t % 4, ip)

                def fox_B(t):
                    ikv, kk, ip = fst.pop(t)
                    mm(PS[FO][:, :], vbuf[ikv][:, kk, :], PT[ip][:], t == 0, t == nkt - 1,
                       [b_vbuf[ikv], b_PT[ip]], [PB[FO]])
                    if t == 0:
                        ve("dve", "tensor_copy", [b_PT[ip]], [b_Lacc], Lacc[:], PT[ip][:])
                    else:
                        ve("dve", "tensor_tensor", [b_PT[ip], b_Lacc], [b_Lacc], Lacc[:], Lacc[:], PT[ip][:],
                           ALU.add)

                def sb_A(ps):
                    JA = nkt - 1 - 2 * ps
                    if JA % 4 == 3:
                        cur["s"] = load_kv(1, JA // 4)
                    ikv = cur["s"]
                    z = ps % 2
                    for half, J in ((0, JA), (1, JA - 1)):
                        bZ = 4 + 2 * z + half
                        dg = J >= 16 * m
                        mm(PS[bZ][:, :], kbuf[ikv][:, (J % 4) * 128:(J % 4 + 1) * 128], qs, True, False,
                           [b_kbuf[ikv], b_big[H + h]], [PB[bZ]], skip_group_check=True)
                        if dg:
                            mm(PS[bZ][:, :], ident_bf, maskt[:, J - 16 * m, 0:512], False, False,
                               [b_const, b_mask], [PB[bZ]], skip_group_check=True)
                    sst[ps] = (ikv, JA % 4, z)

                def sb_exp(ps):
                    ikv, kkA, z = sst[ps]
                    pbz = [PB[4 + 2 * z], PB[5 + 2 * z]]
                    act(E1[:], PSZ[z][:, :], AF.Exp, pbz, [b_E1])

                def sb_ln(ps):
                    ikv, kkA, z = sst[ps]
                    act(SPb[z][:], E1[:], AF.Ln, [b_E1], [b_SPb[z]], bias=1.0)

                def sb_B(ps):
                    ikv, kkA, z = sst[ps]
                    bA = 4 + 2 * z; bB = 5 + 2 * z
                    prev = (ps - 1) % 2
                    mm(PS[bA][:, :], negtri, SPb[z][:, 0:512], False, ps == 0, [b_const, b_SPb[z]], [PB[bA]],
                       skip_group_check=True)
                    if ps >= 1:
                        mm(PS[bA][:, :], negones, SPa[prev][:], False, True, [b_const, b_SPa[prev]], [PB[bA]],
                           skip_group_check=True)
                    mm(PS[bB][:, :], negtri, SPb[z][:, 512:1024], False, False, [b_const, b_SPb[z]], [PB[bB]],
                       skip_group_check=True)
                    mm(PS[bB][:, :], negones, SPb[z][:, 0:512], False, ps == 0, [b_const, b_SPb[z]], [PB[bB]],
                       skip_group_check=True)
                    if ps >= 1:
                        mm(PS[bB][:, :], negones, SPa[prev][:], False, True, [b_const, b_SPa[prev]], [PB[bB]],
                           skip_group_check=True)
                    act(AT[z][:], PSZ[z][:, :], AF.Exp, [PB[bA], PB[bB]], [b_AT[z]])
                    if ps < npair - 1:
                        if ps == 0:
                            ve("pool", "tensor_tensor", [b_SPb[z]], [b_SPa[0]], SPa[0][:], SPb[z][:, 0:512],
                               SPb[z][:, 512:1024], ALU.add)
                        else:
                            ve("pool", "tensor_tensor", [b_SPb[z], b_SPa[prev]], [b_SPa[ps % 2]],
                               SPa[ps % 2][:], SPa[prev][:], SPb[z][:, 0:512], ALU.add)
                            ve("pool", "tensor_tensor", [b_SPb[z], b_SPa[ps % 2]], [b_SPa[ps % 2]],
                               SPa[ps % 2][:], SPa[ps % 2][:], SPb[z][:, 512:1024], ALU.add)

                def sb_C(ps):
                    ikv, kkA, z = sst.pop(ps)
                    mm(PS[SO][:, :], vbuf[ikv][:, kkA, :], AT[z][:, 0:512], ps == 0, False,
                       [b_vbuf[ikv], b_AT[z]], [PB[SO]])
                    mm(PS[SO][:, :], vbuf[ikv][:, kkA - 1, :], AT[z][:, 512:1024], False, ps == npair - 1,
                       [b_vbuf[ikv], b_AT[z]], [PB[SO]])

                def step(ps):
                    if ps < npair:
                        fox_A(2 * ps)
                        sb_A(ps)
                        sb_exp(ps)
                        fox_A(2 * ps + 1)
                    if 1 <= ps <= npair:
                        sb_B(ps - 1)
                    if ps < npair:
                        sb_ln(ps)
                    if 1 <= ps <= npair:
                        fox_B(2 * ps - 1)
                    if ps < npair:
                        fox_B(2 * ps)
                    if ps >= 2:
                        sb_C(ps - 2)
                def fin_f():
                    i = h % 4
                    mm(PS[FL][:, :], ones_f, Lacc[:], True, True, [b_const, b_Lacc], [PB[FL]])
                    ve("dve", "reciprocal", [PB[FL]], [b_tmpf[i]], tmpf[i][:], PS[FL][:, :])
                    ve("dve", "tensor_tensor", [PB[FO], b_tmpf[i]], [b_big[2 * H + h]], big[:, 2 * H + h, :],
                       PS[FO][:, :], tmpf[i][:], ALU.mult)

                def fin_s():
                    ve("dve", "tensor_copy", [PB[SO]], [b_big[3 * H + h]], big[:, 3 * H + h, :], PS[SO][:, :])

                return step, fin_f, fin_s

            npair_m = nkt // 2
            prev = None
            for h in range(H):
                hd = make_head(h)
                for ps in range(npair_m):
                    if prev is not None and ps < 2:
                        prev[0](npair_m + ps)
                        if ps == 0:
                            prev[1]()
                        else:
                            prev[2]()
                    hd[0](ps)
                prev = hd
            prev[0](npair_m); prev[1](); prev[0](npair_m + 1); prev[2]()
            for g in range(NGD):
                wA, bA = wload(wof_s[g], H)
                wB, bB = wload(wos_s[g], H)
                wGa, bGa = wload(wga_s[g], KC)
                wGb, bGb = wload(wgb_s[g], KC)
                for cc in range(GW // 128):
                    c = g * (GW // 128) + cc
                    cs = slice(cc * 128, (cc + 1) * 128)
                    byA = nbank(); byB = nbank(); bgA = nbank(); bgB = nbank()
                    for h in range(H):
                        mm(PS[byA][:, :], wA[:, h, cs], big[:, 2 * H + h, :], h == 0, h == H - 1,
                           [bA, b_big[2 * H + h]], [PB[byA]])
                    for h in range(H):
                        mm(PS[byB][:, :], wB[:, h, cs], big[:, 3 * H + h, :], h == 0, h == H - 1,
                           [bB, b_big[3 * H + h]], [PB[byB]])
                    for kc in range(KC):
                        mm(PS[bgA][:, :], wGa[:, kc, cs], xnT2[:, kc, :], kc == 0, kc == KC - 1,
                           [bGa, b_xn2[kc]], [PB[bgA]])
                    for kc in range(KC):
                        mm(PS[bgB][:, :], wGb[:, kc, cs], xnT2[:, kc, :], kc == 0, kc == KC - 1,
                           [bGb, b_xn2[kc]], [PB[bgB]])
                    act(tmpf[0][:], PS[bgA][:, :], AF.Sigmoid, [PB[bgA]], [b_tmpf[0]])
                    act(tmpf[1][:], PS[bgB][:, :], AF.Sigmoid, [PB[bgB]], [b_tmpf[1]])
                    ve("dve", "tensor_tensor", [PB[byA], b_tmpf[0]], [b_tmpf[2]], tmpf[2][:], PS[byA][:, :],
                       tmpf[0][:], ALU.mult)
                    ve("dve", "tensor_tensor", [PB[byB], b_tmpf[1]], [b_tmpf[3]], tmpf[3][:], PS[byB][:, :],
                       tmpf[1][:], ALU.mult)
                    ve("dve", "tensor_tensor", [b_tmpf[2], b_tmpf[3]], [b_big[c]], big[:, c, :], tmpf[2][:],
                       tmpf[3][:], ALU.add)
            for g in range(NGD):
                wb, bwb = wload(wo_s[g], KC)
                for cc in range(GW // 128):
                    c = g * (GW // 128) + cc
                    bk = nbank()
                    for kc in range(KC):
                        mm(PS[bk][:, :], wb[:, kc, cc * 128:(cc + 1) * 128], big[:, kc, :], kc == 0, kc == KC - 1,
                           [bwb, b_big[kc]], [PB[bk]])
                    ve("dve", "tensor_tensor", [PB[bk], b_xT[c]], [b_xT[c]], xT[:, c, :], xT[:, c, :], PS[bk][:, :],
                       ALU.add)
            rmsnorm_fm(gmlp, xnT2, b_xn2)
            for half in range(2):
                for g in range(HJ * 128 // GW):
                    wb, bwb = wload(wup_s[half * (HJ * 128 // GW) + g], KC)
                    for jj in range(GW // 128):
                        j = g * (GW // 128) + jj
                        bk = nbank()
                        for kc in range(KC):
                            mm(PS[bk][:, :], wb[:, kc, jj * 128:(jj + 1) * 128], xnT2[:, kc, :], kc == 0,
                               kc == KC - 1, [bwb, b_xn2[kc]], [PB[bk]])
                        i = j % 4
                        act(tmpf[i][:], PS[bk][:, :], AF.Square, [PB[bk]], [b_tmpf[i]])
                        ve("dve", "scalar_tensor_tensor", [PB[bk], b_tmpf[i]], [b_big[j]], big[:, j, :],
                           PS[bk][:, :], 0.0, tmpf[i][:], ALU.is_gt, ALU.mult)
                for g in range(NGD):
                    pieces = []
                    step_k = min(KC, HJ)
                    for k0 in range(0, HJ, step_k):
                        wb, bwb = wload(wdn_s[g, :, half * HJ + k0: half * HJ + k0 + step_k, :], step_k)
                        pieces.append((wb, bwb, k0, step_k))
                    for cc in range(GW // 128):
                        c = g * (GW // 128) + cc
                        bk = nbank()
                        n = 0
                        for (wb, bwb, k0, sk) in pieces:
                            for kk in range(sk):
                                mm(PS[bk][:, :], wb[:, kk, cc * 128:(cc + 1) * 128], big[:, k0 + kk, :], n == 0,
                                   n == HJ - 1, [bwb, b_big[k0 + kk]], [PB[bk]])
                                n += 1
                        ve("dve", "tensor_tensor", [PB[bk], b_xT[c]], [b_xT[c]], xT[:, c, :], xT[:, c, :],
                           PS[bk][:, :], ALU.add)
            rmsnorm_fm(gfin, xT, b_xT)
            for sub in range(4):
                i = tctr[0] % 2; tctr[0] += 1
                for k0 in range(0, KC, 4):
                    bk = nbank()
                    kn = min(4, KC - k0)
                    for q in range(kn):
                        tr(PS[bk][:, q * 128:(q + 1) * 128], xT[:, k0 + q, sub * 128:(sub + 1) * 128], ident_f,
                           [b_xT[k0 + q], b_const], [PB[bk]])
                    act(stage[i][:, k0 * 128:(k0 + kn) * 128], PS[bk][:, 0:kn * 128], AF.Copy, [PB[bk]],
                        [b_stage[i]])
                r0 = m * 512 + sub * 128
                P.dma("pool", out_d[r0:r0 + 128, :], stage[i][:], key=f"out{i}", reads=[b_stage[i]],
                      writes=[P.buf()])
        P.barrier()
        P.emit()
    return nc, P


def host_consts(r):
    cbf = np.zeros((128, 4, 128), np.float32)
    cbf[:, 0, :] = np.eye(128)
    j = np.arange(128)[:, None]; s = np.arange(128)[None, :]
    cbf[:, 1, :] = -(j >= s).astype(np.float32)
    cbf[:, 2, :] = -1.0
    cbf[:, 3, :] = 1.0
    cf32 = np.zeros((128, 3, 128), np.float32)
    cf32[:, 0, :] = np.eye(128)
    cf32[:, 1, :] = (j <= s).astype(np.float32)
    cf32[:, 2, :] = 1.0
    sel = np.zeros((16, 4), np.float32)
    for i in range(4):
        sel[4 * r + i, i] = -1.0
    sidx = np.arange(128)[:, None, None]
    idx = np.arange(16)[None, :, None]
    col = np.arange(513)[None, None, :]
    keypos = 128 * idx + sidx
    qpos = 512 * r + col - 1
    masks = np.where(keypos > qpos, NEG, 0.0).astype(np.float32)
    bf = ml_dtypes.bfloat16
    return cbf.astype(bf), cf32, sel, masks.astype(bf)


_CACHE = {}


def run(cfg, x, norm_mix_g, w_in, b_forget, w_out_fox, w_out_sb, w_out, norm_mlp_g, w_mlp_up, w_mlp_down,
        norm_final_g):
    D = cfg["D"]; H = cfg["H"]; NSLOT = cfg["NSLOT"]; KC = D // 128
    S = 2048 * NSLOT
    key = tuple(sorted(cfg.items()))
    if key not in _CACHE:
        _CACHE[key] = build(cfg)
    nc, P = _CACHE[key]
    f = lambda a: np.ascontiguousarray(np.asarray(a, dtype=np.float32))
    x = f(x)

    def col(g):
        return np.ascontiguousarray(f(g).reshape(KC, 128).T)

    in_maps = []
    for c in range(8):
        b = c // 4; r = c % 4
        cbf, cf32, sel, masks = host_consts(r)
        xb = x[b]
        own = np.ascontiguousarray(xb.reshape(NSLOT, 4, 512, D)[:, r].reshape(NSLOT * 512, D))
        in_maps.append(dict(
            x_seq=xb, x_own=own, w_in=f(w_in[0]), w_of=f(w_out_fox[0]), w_os=f(w_out_sb[0]), w_o=f(w_out[0]),
            w_up=f(w_mlp_up[0]), w_dn=f(w_mlp_down[0]), gmix=col(norm_mix_g[0]), gmlp=col(norm_mlp_g[0]),
            gfin=col(norm_final_g), bfg=np.ascontiguousarray(np.broadcast_to(f(b_forget[0])[None, :], (128, H))),
            masks=masks, sel=sel, cbf=cbf, cf32=cf32))
    res = run_bass_kernel_spmd(nc, in_maps, core_ids=list(range(8)))
    out = np.empty((2, S, D), np.float32)
    for c in range(8):
        b = c // 4; r = c % 4
        o = np.asarray(res.results[c]["out"]).reshape(NSLOT, 512, D)
        out[b].reshape(NSLOT, 4, 512, D)[:, r] = o
    return out


def kernel(**inputs):
    return run(FULL, **inputs)
```
